# Optimizing a Trainium2 kernel written in Bass

```python
import jax, jax.numpy as jnp
from jax import lax
import numpy as np

D_MODEL = 1024
BATCH = 16
SEQ = 2048
DEPTH = 2
DEC_BATCH = 16
DEC_SEQ = 32
PAST_LEN = 1024

CHUNK = 64
N_PREV_CHUNKS = 8
BAND_PAST = N_PREV_CHUNKS * CHUNK
BAND = BAND_PAST + CHUNK
HEAD_DIM = 64
D_MIX = D_MODEL
D_A = D_MIX // 2
D_B = D_MIX - D_A
N_HEADS_A = D_A // HEAD_DIM
N_HEADS_B = D_B // HEAD_DIM
REL_CLIP = 128
SB_BLOCK = 128
D_PLE = 256
RMS_EPS = 1e-6
D_IN = 4 * D_A + 4 * D_B
ATTN_SCALE = HEAD_DIM ** -0.5
NEG_INF = -1e30
SPLIT_IDX = [int(v) for v in np.cumsum([D_A] * 4 + [D_B] * 4)[:-1]]

kernel_name = "hybrid_band_stickbreaking_stream_step"


def rmsnorm(x, g):
    x32 = x.astype(jnp.float32)
    y = x32 * lax.rsqrt(jnp.mean(x32 * x32, axis=-1, keepdims=True) + RMS_EPS)
    return (y * g.astype(jnp.float32)).astype(x.dtype)


def heads(t, n):
    return t.reshape(t.shape[:-1] + (n, HEAD_DIM))


def in_proj(h, g_pre, w_in):
    u = rmsnorm(h, g_pre) @ w_in
    qa, ka, va, ga, qb, kb, vb, gb = jnp.split(u, SPLIT_IDX, axis=-1)
    return (heads(qa, N_HEADS_A), heads(ka, N_HEADS_A), heads(va, N_HEADS_A), ga,
            heads(qb, N_HEADS_B), heads(kb, N_HEADS_B), heads(vb, N_HEADS_B), gb)


def band_attention(q, k, v, q_pos, k_pos, bias_table):
    s = jnp.einsum('bqhd,bkhd->bhqk', q, k).astype(jnp.float32) * ATTN_SCALE
    rel = jnp.clip(k_pos[None, :] - q_pos[:, None], -REL_CLIP, REL_CLIP) + REL_CLIP
    s = s + bias_table[:, rel].astype(jnp.float32)[None]
    qc = q_pos // CHUNK
    kc = k_pos // CHUNK
    ok = ((k_pos[None, :] >= 0) & (kc[None, :] <= qc[:, None])
          & (kc[None, :] >= qc[:, None] - N_PREV_CHUNKS))
    s = jnp.where(ok[None, None], s, NEG_INF)
    w = jax.nn.softmax(s, axis=-1).astype(v.dtype)
    return jnp.einsum('bhqk,bkhd->bqhd', w, v)


def stick_breaking(q, k, v, q_pos, k_pos):
    z = jnp.einsum('bqhd,bkhd->bhqk', q, k).astype(jnp.float32) * ATTN_SCALE
    causal = (k_pos[None, :] < q_pos[:, None])[None, None]
    log_beta = jax.nn.log_sigmoid(z)
    log_1m = jnp.where(causal, jax.nn.log_sigmoid(-z), 0.0)
    after = lax.cumsum(log_1m, axis=3, reverse=True) - log_1m
    w = jnp.where(causal, jnp.exp(log_beta + after), 0.0).astype(v.dtype)
    return jnp.einsum('bhqk,bkhd->bqhd', w, v)


def mixer_a_prompt(q, k, v, bias_table):
    b, s = q.shape[0], q.shape[1]
    nc = s // CHUNK
    pad = ((0, 0), (BAND_PAST, 0), (0, 0), (0, 0))
    kp = jnp.pad(k, pad)
    vp = jnp.pad(v, pad)

    def one_chunk(c):
        start = c * CHUNK
        qi = lax.dynamic_slice_in_dim(q, start, CHUNK, axis=1)
        ki = lax.dynamic_slice_in_dim(kp, start, BAND, axis=1)
        vi = lax.dynamic_slice_in_dim(vp, start, BAND, axis=1)
        q_pos = start + jnp.arange(CHUNK, dtype=jnp.int32)
        k_pos = start - BAND_PAST + jnp.arange(BAND, dtype=jnp.int32)
        return band_attention(qi, ki, vi, q_pos, k_pos, bias_table)

    o = lax.map(one_chunk, jnp.arange(nc, dtype=jnp.int32))
    return jnp.moveaxis(o, 0, 1).reshape(b, s, N_HEADS_A * HEAD_DIM)


def mixer_b_prompt(q, k, v):
    b, s = q.shape[0], q.shape[1]
    nb = s // SB_BLOCK
    k_pos = jnp.arange(s, dtype=jnp.int32)

    def one_block(i):
        start = i * SB_BLOCK
        qi = lax.dynamic_slice_in_dim(q, start, SB_BLOCK, axis=1)
        q_pos = start + jnp.arange(SB_BLOCK, dtype=jnp.int32)
        return stick_breaking(qi, k, v, q_pos, k_pos)

    o = lax.map(one_block, jnp.arange(nb, dtype=jnp.int32))
    return jnp.moveaxis(o, 0, 1).reshape(b, s, N_HEADS_B * HEAD_DIM)


def mixer_a_sample(q, k, v, ca_k, ca_v, past, bias_table):
    b, t = q.shape[0], q.shape[1]
    la = ca_k.shape[1]
    k_all = jnp.concatenate([ca_k, k], axis=1)
    v_all = jnp.concatenate([ca_v, v], axis=1)
    k_pos = past - la + jnp.arange(la + t, dtype=jnp.int32)
    q_pos = past + jnp.arange(t, dtype=jnp.int32)
    o = band_attention(q, k_all, v_all, q_pos, k_pos, bias_table)
    return o.reshape(b, t, N_HEADS_A * HEAD_DIM)


def mixer_b_sample(q, k, v, cb_k, cb_v):
    b, t = q.shape[0], q.shape[1]
    past = cb_k.shape[1]
    k_all = jnp.concatenate([cb_k, k], axis=1)
    v_all = jnp.concatenate([cb_v, v], axis=1)
    k_pos = jnp.arange(past + t, dtype=jnp.int32)
    q_pos = past + jnp.arange(t, dtype=jnp.int32)
    o = stick_breaking(q, k_all, v_all, q_pos, k_pos)
    return o.reshape(b, t, N_HEADS_B * HEAD_DIM)


def finish(h, oa, ga, ob, gb, w_out, g_post, p, w_ple, w_ple_gate):
    y = jnp.concatenate([oa * jax.nn.silu(ga), ob * jax.nn.silu(gb)], axis=-1) @ w_out
    h = h + rmsnorm(y, g_post)
    return h + jax.nn.sigmoid(h @ w_ple_gate) * (p @ w_ple)


def setup_inputs(seed: int = 0) -> dict:
    key = jax.random.key(seed)
    ks = jax.random.split(key, 16)
    f32 = jnp.float32
    la = min(BAND_PAST, PAST_LEN)
    nrm = lambda k, shape: jax.random.normal(k, shape, f32)
    return {
        "x_prompt": nrm(ks[0], (BATCH, SEQ, D_MODEL)),
        "x_sample": nrm(ks[1], (DEC_BATCH, DEC_SEQ, D_MODEL)),
        "p_prompt": nrm(ks[2], (DEPTH, BATCH, SEQ, D_PLE)),
        "p_sample": nrm(ks[3], (DEPTH, DEC_BATCH, DEC_SEQ, D_PLE)),
        "cache_a_k": nrm(ks[4], (DEPTH, DEC_BATCH, la, N_HEADS_A, HEAD_DIM)),
        "cache_a_v": nrm(ks[5], (DEPTH, DEC_BATCH, la, N_HEADS_A, HEAD_DIM)),
        "cache_b_k": nrm(ks[6], (DEPTH, DEC_BATCH, PAST_LEN, N_HEADS_B, HEAD_DIM)),
        "cache_b_v": nrm(ks[7], (DEPTH, DEC_BATCH, PAST_LEN, N_HEADS_B, HEAD_DIM)),
        "g_pre": 1.0 + 0.05 * nrm(ks[8], (DEPTH, D_MODEL)),
        "w_in": nrm(ks[9], (DEPTH, D_MODEL, D_IN)) * D_MODEL ** -0.5,
        "rel_bias": 0.1 * nrm(ks[10], (DEPTH, N_HEADS_A, 2 * REL_CLIP + 1)),
        "w_out": nrm(ks[11], (DEPTH, D_MIX, D_MODEL)) * D_MIX ** -0.5,
        "g_post": 1.0 + 0.05 * nrm(ks[12], (DEPTH, D_MODEL)),
        "w_ple": nrm(ks[13], (DEPTH, D_PLE, D_MODEL)) * D_PLE ** -0.5,
        "w_ple_gate": nrm(ks[14], (DEPTH, D_MODEL, D_MODEL)) * D_MODEL ** -0.5,
    }


def reference(x_prompt, x_sample, p_prompt, p_sample, cache_a_k, cache_a_v,
              cache_b_k, cache_b_v, g_pre, w_in, rel_bias, w_out, g_post,
              w_ple, w_ple_gate):
    hp, hs = x_prompt, x_sample
    seq = x_prompt.shape[1]
    past = cache_b_k.shape[2]
    keep = min(BAND_PAST, seq)
    pak, pav, pbk, pbv = [], [], [], []
    sak, sav, sbk, sbv = [], [], [], []
    for i in range(DEPTH):
        qa, ka, va, ga, qb, kb, vb, gb = in_proj(hp, g_pre[i], w_in[i])
        oa = mixer_a_prompt(qa, ka, va, rel_bias[i])
        ob = mixer_b_prompt(qb, kb, vb)
        hp = finish(hp, oa, ga, ob, gb, w_out[i], g_post[i], p_prompt[i], w_ple[i], w_ple_gate[i])
        pak.append(ka[:, seq - keep:])
        pav.append(va[:, seq - keep:])
        pbk.append(kb)
        pbv.append(vb)
        qa, ka, va, ga, qb, kb, vb, gb = in_proj(hs, g_pre[i], w_in[i])
        oa = mixer_a_sample(qa, ka, va, cache_a_k[i], cache_a_v[i], past, rel_bias[i])
        ob = mixer_b_sample(qb, kb, vb, cache_b_k[i], cache_b_v[i])
        hs = finish(hs, oa, ga, ob, gb, w_out[i], g_post[i], p_sample[i], w_ple[i], w_ple_gate[i])
        sak.append(ka)
        sav.append(va)
        sbk.append(kb)
        sbv.append(vb)
    return (hp, hs,
            jnp.stack(pak), jnp.stack(pav), jnp.stack(pbk), jnp.stack(pbv),
            jnp.stack(sak), jnp.stack(sav), jnp.stack(sbk), jnp.stack(sbv))
```

```python
import numpy as np
from contextlib import ExitStack
import concourse.bass as bass
import concourse.mybir as mybir
from concourse.bass_utils import run_bass_kernel_spmd

F32 = mybir.dt.float32
BF16 = mybir.dt.bfloat16
AF = mybir.ActivationFunctionType
ALU = mybir.AluOpType

NCORES = 8
DEPTH = 2
D = 1024
SEQ = 2048
DSEQ = 32
PAST = 1024
NB = 2
NS = 2
EPS = 1e-6
NEG = -30000.0
STRICT = True
QA, KA, VA_, GA, QB, KB, VB_, GB = [512 * i for i in range(8)]


class Buf:
    __slots__ = ("name", "ap", "last_w", "readers", "dma_readers", "bf", "psum")

    def __init__(self, name, ap):
        self.name = name
        self.ap = ap
        self.last_w = None
        self.readers = {}
        self.dma_readers = []
        self.psum = False


class Op:
    __slots__ = ("eng", "fn", "deps", "idx", "signal", "val", "is_dma", "dsem", "dval", "ringwait")

    def __init__(self, eng, fn, is_dma):
        self.eng = eng
        self.fn = fn
        self.deps = []
        self.signal = False
        self.val = 0
        self.is_dma = is_dma
        self.dsem = None
        self.dval = 0
        self.ringwait = None


class Sched:
    RING = {"sp": 20, "pool": 24}

    def __init__(self, nc, es):
        self.nc = nc
        self.es = es
        self.engs = {"pe": nc.tensor, "act": nc.scalar, "dve": nc.vector, "pool": nc.gpsimd, "sp": nc.sync}
        self.ops = {k: [] for k in self.engs}
        self.sem = {k: es.enter_context(nc.semaphore("s_" + k)) for k in self.engs}
        self.ring = {q: [es.enter_context(nc.semaphore("r_%s%d" % (q, i))) for i in range(n)] for q, n in self.RING.items()}
        self.ndma = {q: 0 for q in self.RING}

    def sb(self, name, shape, dtype):
        return Buf(name, self.nc.alloc_sbuf_tensor(name, list(shape), dtype))

    def ps(self, name, shape, dtype):
        b = Buf(name, self.nc.alloc_psum_tensor(name, list(shape), dtype))
        b.psum = True
        return b

    def _adddeps(self, op, reads, writes):
        deps = []
        for b in reads:
            w = b.last_w
            if w is not None:
                if w.eng == op.eng and not w.is_dma and not op.is_dma:
                    if op.eng != "pe":
                        deps.append(w)
                else:
                    deps.append(w)
            if b.psum:
                for e, r in b.readers.items():
                    if e != op.eng:
                        deps.append(r)
        strict = STRICT and op.eng != "pe"
        for b in writes:
            w = b.last_w
            if w is not None and (w.is_dma or op.is_dma or w.eng != op.eng or strict):
                deps.append(w)
            for e, r in b.readers.items():
                if r.is_dma or op.is_dma or e != op.eng or strict:
                    deps.append(r)
            deps.extend(b.dma_readers)
        seen = set()
        for d in deps:
            if id(d) not in seen and d is not op:
                seen.add(id(d))
                op.deps.append(d)
                if not d.is_dma:
                    d.signal = True
        for b in writes:
            b.last_w = op
            b.readers = {}
            b.dma_readers = []
        for b in reads:
            if b.last_w is op:
                continue
            if op.is_dma:
                b.dma_readers.append(op)
            else:
                b.readers[op.eng] = op

    def op(self, eng, fn, reads=(), writes=()):
        o = Op(eng, fn, False)
        self.ops[eng].append(o)
        self._adddeps(o, reads, writes)
        return o

    def dma(self, q, out, in_, reads=(), writes=(), **kw):
        def fn(e, out=out, in_=in_, kw=kw):
            return e.dma_start(out=out, in_=in_, **kw)

        o = Op(q, fn, True)
        self.ops[q].append(o)
        n = self.ndma[q]
        self.ndma[q] = n + 1
        R = len(self.ring[q])
        o.dsem = self.ring[q][n % R]
        o.dval = 16 * (n // R + 1)
        if n >= R:
            o.ringwait = (o.dsem, 16 * (n // R))
        self._adddeps(o, reads, writes)
        return o

    def alias(self, newbufs, oldbufs):
        for nb in newbufs:
            for ob in oldbufs:
                w = ob.last_w
                if w is not None:
                    if w.is_dma:
                        nb.dma_readers.append(w)
                    else:
                        nb.readers["w_" + ob.name + w.eng] = w
                for e, r in ob.readers.items():
                    nb.readers["r_" + ob.name + str(e)] = r
                nb.dma_readers.extend(ob.dma_readers)

    def finish(self):
        for k, lst in self.ops.items():
            c = 0
            for o in lst:
                if o.signal:
                    c += 1
                    o.val = c
        for k, lst in self.ops.items():
            e = self.engs[k]
            waited = {}

            def wait(sem, val):
                key = id(sem)
                if waited.get(key, 0) >= val:
                    return
                waited[key] = val
                e.wait_ge(sem, val)

            for o in lst:
                if o.ringwait is not None:
                    wait(*o.ringwait)
                for d in o.deps:
                    if d.is_dma:
                        wait(d.dsem, d.dval)
                    else:
                        wait(self.sem[d.eng], d.val)
                inst = o.fn(e)
                if o.is_dma:
                    inst.then_inc(o.dsem, 16)
                elif o.signal:
                    inst.then_inc(self.sem[k], 1)
            if k in self.ring:
                n = self.ndma[k]
                R = len(self.ring[k])
                for i in range(min(n, R)):
                    wait(self.ring[k][i], 16 * ((n - 1 - i) // R + 1))


def build_program():
    nc = bass.Bass("TRN2", target_bir_lowering=False, dynamic_dma_scratch_size=4096)
    es = ExitStack()
    S = Sched(nc, es)

    def din(name, shape):
        return nc.dram_tensor(name, list(shape), F32, kind="ExternalInput").ap()

    def dout(name, shape):
        return nc.dram_tensor(name, list(shape), F32, kind="ExternalOutput").ap()

    x_prompt = din("x_prompt", [NB, SEQ, D])
    x_sample = din("x_sample", [NS, DSEQ, D])
    p_prompt = din("p_prompt", [DEPTH, NB, SEQ, 256])
    p_sample = din("p_sample", [DEPTH, NS, DSEQ, 256])
    cache_a_k = din("cache_a_k", [DEPTH, NS, 512, 512])
    cache_a_v = din("cache_a_v", [DEPTH, NS, 512, 512])
    cache_b_k = din("cache_b_k", [DEPTH, NS, PAST, 512])
    cache_b_v = din("cache_b_v", [DEPTH, NS, PAST, 512])
    g_pre = din("g_pre", [DEPTH, D])
    g_post = din("g_post", [DEPTH, D])
    w_in = din("w_in", [DEPTH, D, 4096])
    w_out = din("w_out", [DEPTH, D, D])
    w_ple = din("w_ple", [DEPTH, 256, D])
    w_gate = din("w_ple_gate", [DEPTH, D, D])
    bias_t = din("bias_t", [DEPTH, 128, 8 * 2 * 128])
    bias_c = din("bias_c", [DEPTH, 8])

    y_prompt = dout("y_prompt", [NB, SEQ, D])
    y_sample = dout("y_sample", [NS, DSEQ, D])
    pak = dout("pak", [DEPTH, NB, 512, 512])
    pav = dout("pav", [DEPTH, NB, 512, 512])
    pbk = dout("pbk", [DEPTH, NB, SEQ, 512])
    pbv = dout("pbv", [DEPTH, NB, SEQ, 512])
    sak = dout("sak", [DEPTH, NS, DSEQ, 512])
    sav = dout("sav", [DEPTH, NS, DSEQ, 512])
    sbk = dout("sbk", [DEPTH, NS, DSEQ, 512])
    sbv = dout("sbv", [DEPTH, NS, DSEQ, 512])
    h1p = nc.dram_tensor("h1p", [NB, SEQ, D], F32).ap()
    h1s = nc.dram_tensor("h1s", [NS, DSEQ, D], F32).ap()

    Win = S.sb("Win", [128, 8, 4096], BF16)
    Wout = S.sb("Wout", [128, 8, 1024], BF16)
    Wgate = S.sb("Wgate", [128, 8, 1024], BF16)
    Wple = S.sb("Wple", [128, 2, 1024], BF16)
    KTB = S.sb("KTB", [128, 4, 2048], BF16)
    VB = S.sb("VB", [128, 16, 512], BF16)
    KTA = S.sb("KTA", [128, 4, 1024], BF16)
    VA = S.sb("VA", [128, 8, 512], BF16)
    hnT = S.sb("hnT", [128, 8, 512], BF16)
    QT = S.sb("QT", [128, 16, 512], BF16)
    sgT = S.sb("sgT", [128, 8, 512], BF16)
    gpre = S.sb("gpre", [128, 1024], F32)
    gpost = S.sb("gpost", [128, 1024], F32)
    biasT = S.sb("biasT", [128, 8, 2, 128], BF16)
    bias4 = S.sb("bias4", [128, 128], BF16)
    cvec = S.sb("cvec", [128, 8], F32)
    ident = S.sb("ident", [128, 128], BF16)
    triM = S.sb("triM", [128, 128], BF16)
    nones = S.sb("nones", [128, 128], BF16)
    onesLR = S.sb("onesLR", [128, 2, 128], BF16)
    mdiag = S.sb("mdiag", [128, 128], BF16)
    hf = S.sb("hf", [128, 1024], F32)
    hn = S.sb("hn", [128, 1024], BF16)
    stage = [S.sb("stage%d" % i, [128, 512], F32) for i in range(2)]
    kbf = S.sb("kbf", [128, 512], BF16)
    st_ss = S.sb("st_ss", [128, 2], F32)
    st_s1 = S.sb("st_s1", [128, 1], F32)
    st_ln = S.sb("st_ln", [128, 1], F32)
    st_r = S.sb("st_r", [128, 1], F32)
    XB = 14336
    X = nc.alloc_sbuf_tensor("X", [128, XB], mybir.dt.uint8)
    LAYOUTS = {
        "p1": [("h1", [128, 1024], F32), ("h2", [128, 1024], F32), ("h3", [128, 1024], F32), ("kbf2", [128, 512], BF16)],
        "att": [("e32_0", [128, 512], F32), ("e32_1", [128, 512], F32), ("spb_0", [128, 512], BF16), ("spb_1", [128, 512], BF16),
                ("spb_2", [128, 512], BF16), ("spb_3", [128, 512], BF16), ("PT_0", [128, 512], BF16), ("PT_1", [128, 512], BF16),
                ("PT_2", [128, 512], BF16), ("PT_3", [128, 512], BF16), ("SS_0", [128, 512], BF16), ("SS_1", [128, 512], BF16)],
        "p3": [("hfB", [128, 1024], F32), ("sig", [128, 512], F32), ("hT_0", [128, 8, 128], BF16), ("hT_1", [128, 8, 128], BF16),
               ("pf_0", [128, 256], F32), ("pf_1", [128, 256], F32), ("pb", [128, 256], BF16), ("pT_0", [128, 2, 128], BF16), ("pT_1", [128, 2, 128], BF16)],
    }
    VIEWS = {}
    for ph, lst in LAYOUTS.items():
        o = 0
        VIEWS[ph] = {}
        for nm, shp, dt in lst:
            n = int(np.prod(shp[1:])) * (4 if dt == F32 else 2)
            ap = X[:, o:o + n].bitcast(dt)
            if len(shp) == 3:
                ap = ap.rearrange("p (a b) -> p a b", a=shp[1])
            VIEWS[ph][nm] = ap
            VIEWS[ph][nm + "@off"] = o
            o += n
        assert o <= XB, (ph, o)
    state = {"xbufs": [], "sample": False}

    def xpair(ph, nm, dt):
        o = VIEWS[ph][nm + "@off"]
        nbytes = 2 * 512 * (4 if dt == F32 else 2)
        return X[:, o:o + nbytes].bitcast(dt).rearrange("p (a b) -> p a b", a=2)

    def phase_views(ph):
        bufs = {nm: Buf(nm, ap) for nm, ap in VIEWS[ph].items() if not nm.endswith("@off")}
        S.alias(list(bufs.values()), state["xbufs"])
        state["xbufs"] = list(bufs.values())
        return bufs

    BIG = nc.alloc_psum_tensor("BIG", [128, 8, 512], F32)
    BK = []
    for i in range(8):
        bk = Buf("BK%d" % i, BIG[:, i, :])
        bk.psum = True
        bk.bf = BIG[:, i, :].bitcast(BF16)
        BK.append(bk)
    EB = BK[0:3]
    PO2 = [BK[3], BK[4]]
    PDEN2 = [BK[5], BK[6]]
    dense_banks = BK[0:6]
    PTR = [BK[6], BK[7]]
    rot = {"d": 0, "s": 0, "t": 0, "u": 0}

    def next_bank():
        b = dense_banks[rot["d"] % len(dense_banks)]
        rot["d"] += 1
        return b

    def next_stage():
        b = stage[rot["s"] % 2]
        rot["s"] += 1
        return b

    def next_ptr():
        b = PTR[rot["t"] % 2]
        rot["t"] += 1
        return b

    hnT_s = [Buf("hnT_s%d" % i, hnT.ap) for i in range(4)]
    st = [{k: S.sb("st_%s%d" % (k, i), [128, 2], F32) for k in ("ss", "s1", "ln", "r")} for i in range(2)]

    S.op("pool", lambda e: e.memset(ident.ap[:], 0.0), writes=[ident])
    S.op("pool", lambda e: e.affine_select(out=ident.ap[:], in_=ident.ap[:], pattern=[[-1, 128]], compare_op=ALU.not_equal,
                                           fill=1.0, base=0, channel_multiplier=1), reads=[ident], writes=[ident])
    S.op("pool", lambda e: e.memset(triM.ap[:], -1.0), writes=[triM])
    S.op("pool", lambda e: e.affine_select(out=triM.ap[:], in_=triM.ap[:], pattern=[[-1, 128]], compare_op=ALU.is_ge,
                                           fill=0.0, base=0, channel_multiplier=1), reads=[triM], writes=[triM])
    S.op("pool", lambda e: e.memset(mdiag.ap[:], 1.0), writes=[mdiag])
    S.op("pool", lambda e: e.affine_select(out=mdiag.ap[:], in_=mdiag.ap[:], pattern=[[1, 128]], compare_op=ALU.is_ge,
                                           fill=0.0, base=-1, channel_multiplier=-1), reads=[mdiag], writes=[mdiag])
    mdiag8 = S.sb("mdiag8", [128, 8 * DSEQ], BF16)
    for h in range(8):
        S.op("pool", lambda e, h=h: e.tensor_copy(out=mdiag8.ap[:, DSEQ * h:DSEQ * (h + 1)], in_=mdiag.ap[:, 0:DSEQ]), reads=[mdiag], writes=[mdiag8])
    S.op("pool", lambda e: e.memset(nones.ap[:], -1.0), writes=[nones])
    S.op("pool", lambda e: e.memset(onesLR.ap[:], 0.0), writes=[onesLR])
    S.op("pool", lambda e: e.memset(onesLR.ap[:, 0, 0:64], 1.0), reads=[onesLR], writes=[onesLR])
    S.op("pool", lambda e: e.memset(onesLR.ap[:, 1, 64:128], 1.0), reads=[onesLR], writes=[onesLR])
    S.op("pool", lambda e: e.memset(QT.ap[:], 0.0), writes=[QT])
    S.op("pool", lambda e: e.memset(bias4.ap[:], 0.0), writes=[bias4])
    S.op("pool", lambda e: e.memset(bias4.ap[0:64, 64:128], NEG), reads=[bias4], writes=[bias4])

    def load_win(l):
        for kc in range(8):
            for hlf in range(2):
                S.dma("pool", Win.ap[:, kc, hlf * 2048:(hlf + 1) * 2048], w_in[l, kc * 128:(kc + 1) * 128, hlf * 2048:(hlf + 1) * 2048], writes=[Win])

    def load_rest(l):
        for kc in range(8):
            S.dma("pool", Wout.ap[:, kc, :], w_out[l, kc * 128:(kc + 1) * 128, :], writes=[Wout])
            S.dma("pool", Wgate.ap[:, kc, :], w_gate[l, kc * 128:(kc + 1) * 128, :], writes=[Wgate])
        for kc in range(2):
            S.dma("pool", Wple.ap[:, kc, :], w_ple[l, kc * 128:(kc + 1) * 128, :], writes=[Wple])
        S.dma("sp", gpre.ap[:], g_pre[l:l + 1, :].partition_broadcast(128), writes=[gpre])
        S.dma("sp", gpost.ap[:], g_post[l:l + 1, :].partition_broadcast(128), writes=[gpost])
        S.dma("sp", cvec.ap[:], bias_c[l:l + 1, :].partition_broadcast(128), writes=[cvec])
        v = phase_views("p1")
        tmp = v["h1"]
        for hh in range(2):
            S.dma("sp", tmp.ap[:], bias_t[l, :, hh * 1024:(hh + 1) * 1024], writes=[tmp])
            for h4 in range(4):
                h = hh * 4 + h4
                S.op("dve", lambda e, h=h, h4=h4: e.tensor_scalar(out=biasT.ap[:, h].rearrange("p a b -> p (a b)"), in0=tmp.ap[:, h4 * 256:(h4 + 1) * 256],
                                                               scalar1=cvec.ap[:, h:h + 1], scalar2=None, op0=ALU.subtract), reads=[tmp, cvec], writes=[biasT])
        S.op("dve", lambda e: e.memset(biasT.ap[64:128, :, 0, 0:64], NEG), reads=[biasT], writes=[biasT])

    def transpose_to(src, n, nchunk, dst_fn, dst_bufs, ptr=None):
        if ptr is None:
            ptr = next_ptr()
        for c in range(nchunk):
            S.op("pe", lambda e, c=c, ptr=ptr: e.transpose(out=ptr.bf[:, c * 128:c * 128 + n], in_=src.ap[0:n, c * 128:(c + 1) * 128],
                                                           identity=ident.ap[0:n, 0:n]), reads=[src, ident], writes=[ptr])
        S.op("dve", lambda e, ptr=ptr: e.tensor_copy(out=dst_fn(), in_=ptr.bf[:, 0:nchunk * 128].rearrange("p (c t) -> p c t", c=nchunk)[:, :, 0:n]),
             reads=[ptr], writes=dst_bufs)

    def rstd_from(sd, ssq_key, n):
        S.op("act", lambda e: e.activation(out=sd["ln"].ap[0:n, 0:1], in_=sd[ssq_key].ap[0:n, 0:1], func=AF.Ln, scale=1.0 / D, bias=EPS),
             reads=[sd[ssq_key]], writes=[sd["ln"]])
        S.op("act", lambda e: e.activation(out=sd["r"].ap[0:n, 0:1], in_=sd["ln"].ap[0:n, 0:1], func=AF.Exp, scale=-0.5), reads=[sd["ln"]], writes=[sd["r"]])

    def pipeline(units, nst, order=None, skews=None):
        n = len(units)
        if order is None:
            order = list(range(nst - 1, -1, -1))
        if skews is None:
            skews = list(range(nst))
        for t in range(n + max(skews)):
            for s_ in order:
                u = t - skews[s_]
                if 0 <= u < n and units[u][s_] is not None:
                    units[u][s_]()

    def phase1(l, j0, subs, hsrc, kvout):
        nsub = len(subs)
        W = 128 * (nsub - 1) + subs[-1]
        v = phase_views("p1")
        hb = [hf, v["h1"], v["h2"], v["h3"]]
        hnb = [hn, hn]
        kbfs = [kbf, v["kbf2"]]
        for i, n in enumerate(subs):
            S.dma("sp", hb[i].ap[0:n], hsrc(i), writes=[hb[i]])

        def rmsA(i):
            n = subs[i]
            sd = st[i % 2]
            h_, hn_ = hb[i], hnb[i % 2]
            S.op("act", lambda e: e.activation(out=sgT.ap[0:n, 0:2, :].rearrange("p a b -> p (a b)"), in_=h_.ap[0:n], func=AF.Square,
                                               accum_out=sd["ss"].ap[0:n, 0:1]), reads=[h_], writes=[sgT, sd["ss"]])
            rstd_from(sd, "ss", n)
            S.op("dve", lambda e: e.scalar_tensor_tensor(out=hn_.ap[0:n], in0=h_.ap[0:n], scalar=sd["r"].ap[0:n, 0:1], in1=gpre.ap[0:n],
                                                         op0=ALU.mult, op1=ALU.mult), reads=[h_, sd["r"], gpre], writes=[hn_])

        def rmsT(i):
            n = subs[i]
            transpose_to(hnb[i % 2], n, 8, lambda: hnT.ap[:, :, 128 * i:128 * i + n], [hnT_s[i]])

        def kv(i):
            n = subs[i]
            kb_ = j0 + i
            outs = kvout(i)
            deferred = []
            for gi, (nm, col) in enumerate([("ka", KA), ("va", VA_), ("kb", KB), ("vb", VB_)]):
                bank = next_bank()
                for kc in range(8):
                    S.op("pe", lambda e, kc=kc, col=col, bank=bank: e.matmul(out=bank.ap[0:n, :], lhsT=hnT.ap[:, kc, 128 * i:128 * i + n],
                                                                              rhs=Win.ap[:, kc, col:col + 512], start=(kc == 0), stop=(kc == 7)),
                         reads=[Win, hnT_s[i]], writes=[bank])
                src_b, src_ap = bank, (lambda bank=bank: bank.ap[0:n, :])
                if outs.get(nm) is not None:
                    stg = next_stage()
                    S.op("act", lambda e, bank=bank, stg=stg: e.activation(out=stg.ap[0:n], in_=bank.ap[0:n, :], func=AF.Copy), reads=[bank], writes=[stg])
                    S.dma("sp", outs[nm], stg.ap[0:n], reads=[stg])
                    src_b, src_ap = stg, (lambda stg=stg: stg.ap[0:n])
                if nm == "va":
                    S.op("dve", lambda e, src_ap=src_ap, s=kb_ % 8: e.tensor_copy(out=VA.ap[0:n, s, :], in_=src_ap()), reads=[src_b], writes=[VA])
                elif nm == "vb":
                    S.op("dve", lambda e, src_ap=src_ap, s=kb_: e.tensor_copy(out=VB.ap[0:n, s, :], in_=src_ap()), reads=[src_b], writes=[VB])
                else:
                    kb2 = kbfs[gi // 2]
                    S.op("dve", lambda e, src_ap=src_ap, kb2=kb2: e.tensor_copy(out=kb2.ap[0:n], in_=src_ap()), reads=[src_b], writes=[kb2])
                    if nm == "ka":
                        c0 = (kb_ % 8) * 128
                        deferred.append(lambda kb2=kb2, c0=c0: transpose_to(kb2, n, 4, lambda: KTA.ap[:, :, c0:c0 + n], [KTA]))
                    else:
                        c0 = kb_ * 128
                        deferred.append(lambda kb2=kb2, c0=c0: transpose_to(kb2, n, 4, lambda: KTB.ap[:, :, c0:c0 + n], [KTB]))
            for f_ in deferred:
                f_()

        rmsA(0)
        rmsT(0)
        for i in range(nsub):
            if i + 1 < nsub:
                rmsA(i + 1)
            kv(i)
            if i + 1 < nsub:
                rmsT(i + 1)
        hs = hnT_s[0:nsub]
        for (col, dst, di, fn_, sc) in (
                [(QA + 128 * c, QT, c, AF.Copy, 0.125) for c in range(4)] + [(QB + 128 * c, QT, 4 + c, AF.Copy, 0.125) for c in range(4)] +
                [(GA + 128 * c, sgT, c, AF.Silu, 1.0) for c in range(4)] + [(GB + 128 * c, sgT, 4 + c, AF.Silu, 1.0) for c in range(4)]):
            bank = next_bank()
            for kc in range(8):
                S.op("pe", lambda e, kc=kc, col=col, bank=bank: e.matmul(out=bank.ap[:, 0:W], lhsT=Win.ap[:, kc, col:col + 128], rhs=hnT.ap[:, kc, 0:W],
                                                                          start=(kc == 0), stop=(kc == 7)), reads=[Win] + hs, writes=[bank])
            if fn_ == AF.Copy:
                for e_ in range(2):
                    S.op("dve", lambda e, bank=bank, dst=dst, di=di, sc=sc, e_=e_: e.tensor_scalar(out=dst.ap[64 * e_:64 * e_ + 64, 2 * di + e_, 0:W],
                                                                                               in0=bank.ap[64 * e_:64 * e_ + 64, 0:W], scalar1=sc, scalar2=None, op0=ALU.mult),
                         reads=[bank], writes=[dst])
            else:
                S.op("act", lambda e, bank=bank, dst=dst, di=di, fn_=fn_, sc=sc: e.activation(out=dst.ap[:, di, 0:W], in_=bank.ap[:, 0:W], func=fn_, scale=sc),
                     reads=[bank], writes=[dst])

    def attention(j0, subs):
        nsub = len(subs)
        W = 128 * (nsub - 1) + subs[-1]
        v = phase_views("att")
        e32 = [v["e32_0"], v["e32_1"]]
        PTb = [v["PT_0"], v["PT_1"], v["PT_2"]]
        hs = hnT_s[0:nsub]

        def nkeys(kb):
            return subs[kb - j0] if kb >= j0 else 128

        units = []
        kb_lo = 4 if state["sample"] else max(j0 - 4, 0)
        for hp in range(4):
            PO, PDEN = ([BK[2], BK[3]], BK[4]) if hp % 2 == 0 else ([BK[5], BK[6]], BK[7])
            ulist = []
            for kb in range(kb_lo, j0 + nsub):
                jlo = max(kb, j0)
                jhi = min(kb + 4, j0 + nsub - 1)
                if jlo > jhi:
                    continue
                for h in (2 * hp, 2 * hp + 1):
                    ulist.append((h, kb, jlo, jhi))
            seen_h = set()
            for ui, (h, kb, jlo, jhi) in enumerate(ulist):
                first = h not in seen_h
                seen_h.add(h)
                last = ui == len(ulist) - 1
                po = 64 * (h % 2)
                nk = nkeys(kb)
                c0 = 128 * (jlo - j0)
                c1 = 128 * (jhi - j0) + subs[jhi - j0]
                slot = kb % 8
                u = rot["u"]
                rot["u"] += 1
                E = EB[u % 2]
                P = PTb[u % 3]
                first_pair = ui == 0

                def s0(E=E, nk=nk, c0=c0, c1=c1, slot=slot, po=po, hp=hp):
                    S.op("pe", lambda e: e.matmul(out=E.ap[0:nk, c0:c1], lhsT=KTA.ap[:, hp, slot * 128:slot * 128 + nk],
                                                  rhs=QT.ap[:, 2 * hp + po // 64, c0:c1], start=True, stop=True), reads=[KTA, QT], writes=[E])

                def s0b(E=E, nk=nk, h=h, kb=kb, jlo=jlo, jhi=jhi):
                    js = list(range(jlo, jhi + 1))
                    if kb in js and kb + 1 in js and subs[kb - j0] == 128 and subs[kb + 1 - j0] == 128:
                        cj = 128 * (kb - j0)
                        S.op("pe", lambda e, cj=cj: e.matmul(out=E.ap[0:nk, cj:cj + 256], lhsT=ident.ap[0:nk, 0:nk],
                                                             rhs=biasT.ap[0:nk, h].rearrange("p a b -> p (a b)"), start=False, stop=True, skip_group_check=True),
                             reads=[ident, biasT], writes=[E])
                        js = [j for j in js if j not in (kb, kb + 1)]
                    for j in js:
                        d = j - kb
                        if d not in (0, 1, 4):
                            continue
                        cj = 128 * (j - j0)
                        nj = subs[j - j0]
                        if d == 4:
                            S.op("pe", lambda e, cj=cj, nj=nj: e.matmul(out=E.ap[0:nk, cj:cj + nj], lhsT=ident.ap[0:nk, 0:nk], rhs=bias4.ap[0:nk, 0:nj],
                                                                        start=False, stop=True, skip_group_check=True), reads=[ident, bias4], writes=[E])
                        else:
                            S.op("pe", lambda e, cj=cj, nj=nj, d=d: e.matmul(out=E.ap[0:nk, cj:cj + nj], lhsT=ident.ap[0:nk, 0:nk], rhs=biasT.ap[0:nk, h, d, 0:nj],
                                                                             start=False, stop=True, skip_group_check=True), reads=[ident, biasT], writes=[E])

                def s1(E=E, P=P, nk=nk, c0=c0, c1=c1, h=h):
                    S.op("act", lambda e: e.activation(out=P.ap[0:nk, c0:c1], in_=E.ap[0:nk, c0:c1], func=AF.Exp), reads=[E], writes=[P])

                def s2(P=P, nk=nk, c0=c0, c1=c1, slot=slot, po=po, h=h, first=first, PO=PO, PDEN=PDEN, first_pair=first_pair):
                    hp_ = h // 2
                    PO_, PD_ = PO[h % 2], PDEN
                    S.op("pe", lambda e: e.matmul(out=PO_.ap[:, c0:c1], lhsT=VA.ap[0:nk, slot, hp_ * 128:(hp_ + 1) * 128], rhs=P.ap[0:nk, c0:c1],
                                                  start=first, stop=True, skip_group_check=True), reads=[VA, P], writes=[PO_])
                    S.op("pe", lambda e: e.matmul(out=PD_.ap[:, c0:c1], lhsT=onesLR.ap[0:nk, h % 2, :], rhs=P.ap[0:nk, c0:c1],
                                                  start=first_pair, stop=True, skip_group_check=True), reads=[onesLR, P], writes=[PD_])

                s3 = None
                if last:
                    def s3(hp=hp, PO=PO, PDEN=PDEN):
                        R = e32[hp % 2]
                        S.op("dve", lambda e: e.reciprocal(out=R.ap[:, 0:W], in_=PDEN.ap[:, 0:W]), reads=[PDEN], writes=[R])
                        for e_ in range(2):
                            S.op("dve", lambda e, e_=e_: e.tensor_tensor(out=R.ap[64 * e_:64 * e_ + 64, 0:W], in0=PO[e_].ap[64 * e_:64 * e_ + 64, 0:W],
                                                                        in1=R.ap[64 * e_:64 * e_ + 64, 0:W], op=ALU.mult), reads=[PO[e_], R], writes=[R])
                        S.op("dve", lambda e: e.tensor_tensor(out=hnT.ap[:, hp, 0:W], in0=R.ap[:, 0:W], in1=sgT.ap[:, hp, 0:W], op=ALU.mult),
                             reads=[R, sgT], writes=hs)
                units.append([(lambda a=s0, b=s0b: (a(), b())), s1, None, s2, None, s3])

        pipeline(units, 6, order=[1, 2, 3, 4, 5, 0], skews=[0, 1, 1, 2, 2, 3])
        units = []
        eeP = xpair("att", "e32_0", F32)
        spP = [xpair("att", "spb_0", BF16), xpair("att", "spb_2", BF16)]
        ptP = [xpair("att", "PT_0", BF16), xpair("att", "PT_2", BF16)]
        ssP = xpair("att", "SS_0", BF16)
        spB = [[v["spb_0"], v["spb_1"]], [v["spb_2"], v["spb_3"]]]
        ptB = [[v["PT_0"], v["PT_1"]], [v["PT_2"], v["PT_3"]]]
        ssB = [v["SS_0"], v["SS_1"]]
        for hp in range(4):
            PO = [BK[6], BK[7]]
            kbs = list(range(j0 + nsub - 1, -1, -1))
            for ui, kb in enumerate(kbs):
                first = ui == 0
                last = ui == len(kbs) - 1
                nk = nkeys(kb)
                diag = kb >= j0
                c0 = 128 * (kb - j0) if diag else 0
                nd = subs[kb - j0] if diag else 0
                u = rot["u"]
                rot["u"] += 1
                q = u % 2
                qe = u % 3
                E2 = [BK[2 * qe], BK[2 * qe + 1]]
                Eap = BIG[0:nk, 2 * qe:2 * qe + 2, c0:W]
                sp2, spap = spB[q], spP[q]
                pt2, ptap = ptB[q], ptP[q]

                def s0(E2=E2, nk=nk, c0=c0, kb=kb, hp=hp):
                    for e_ in range(2):
                        S.op("pe", lambda e, e_=e_: e.matmul(out=E2[e_].ap[0:nk, c0:W], lhsT=KTB.ap[:, hp, kb * 128:kb * 128 + nk],
                                                             rhs=QT.ap[:, 8 + 2 * hp + e_, c0:W], start=True, stop=True), reads=[KTB, QT], writes=[E2[e_]])

                def s1a(E2=E2, Eap=Eap, nk=nk, c0=c0):
                    S.op("act", lambda e: e.activation(out=eeP[0:nk, :, c0:W], in_=Eap, func=AF.Exp), reads=E2, writes=e32)

                def s1b(sp2=sp2, spap=spap, nk=nk, c0=c0, nd=nd, diag=diag):
                    S.op("act", lambda e: e.activation(out=spap[0:nk, :, c0:W], in_=eeP[0:nk, :, c0:W], func=AF.Ln, bias=1.0), reads=e32, writes=sp2)
                    if diag:
                        for e_ in range(2):
                            S.op("dve", lambda e, e_=e_: e.tensor_tensor(out=sp2[e_].ap[0:nk, c0:c0 + nd], in0=sp2[e_].ap[0:nk, c0:c0 + nd],
                                                                        in1=mdiag.ap[0:nk, 0:nd], op=ALU.mult), reads=[sp2[e_], mdiag], writes=[sp2[e_]])

                def s2a(E2=E2, sp2=sp2, spap=spap, nk=nk, c0=c0, kb=kb, first=first):
                    for e_ in range(2):
                        S.op("pe", lambda e, e_=e_: e.matmul(out=E2[e_].ap[0:nk, c0:W], lhsT=triM.ap[0:nk, 0:nk], rhs=sp2[e_].ap[0:nk, c0:W], start=False, stop=True,
                                                             skip_group_check=True), reads=[triM, sp2[e_]], writes=[E2[e_]])
                    if not first:
                        for e_ in range(2):
                            S.op("pe", lambda e, e_=e_: e.matmul(out=E2[e_].ap[0:nk, c0:W], lhsT=nones.ap[:, 0:nk], rhs=ssB[e_].ap[:, c0:W], start=False, stop=True,
                                                                 skip_group_check=True), reads=[nones, ssB[e_]], writes=[E2[e_]])
                    if kb > 0:
                        if first:
                            S.op("dve", lambda e: e.memset(ssP[:, :, 0:W], 0.0), writes=ssB)
                        S.op("dve", lambda e: e.tensor_tensor(out=ssP[0:nk, :, c0:W], in0=ssP[0:nk, :, c0:W], in1=spap[0:nk, :, c0:W], op=ALU.add),
                             reads=ssB + sp2, writes=ssB)

                def s2b(E2=E2, Eap=Eap, pt2=pt2, ptap=ptap, nk=nk, c0=c0, nd=nd, diag=diag):
                    S.op("act", lambda e: e.activation(out=ptap[0:nk, :, c0:W], in_=Eap, func=AF.Exp), reads=E2, writes=pt2)
                    if diag:
                        for e_ in range(2):
                            S.op("dve", lambda e, e_=e_: e.tensor_tensor(out=pt2[e_].ap[0:nk, c0:c0 + nd], in0=pt2[e_].ap[0:nk, c0:c0 + nd],
                                                                        in1=mdiag.ap[0:nk, 0:nd], op=ALU.mult), reads=[pt2[e_], mdiag], writes=[pt2[e_]])

                def s3(pt2=pt2, nk=nk, c0=c0, kb=kb, hp=hp, first=first, last=last, PO=PO):
                    for e_ in range(2):
                        S.op("pe", lambda e, e_=e_: e.matmul(out=PO[e_].ap[:, c0:W], lhsT=VB.ap[0:nk, kb, hp * 128:(hp + 1) * 128], rhs=pt2[e_].ap[0:nk, c0:W],
                                                             start=first, stop=True, skip_group_check=True), reads=[VB, pt2[e_]], writes=[PO[e_]])
                    if last:
                        for e_ in range(2):
                            S.op("dve", lambda e, e_=e_: e.tensor_tensor(out=hnT.ap[64 * e_:64 * e_ + 64, 4 + hp, 0:W], in0=PO[e_].ap[64 * e_:64 * e_ + 64, 0:W],
                                                                        in1=sgT.ap[64 * e_:64 * e_ + 64, 4 + hp, 0:W], op=ALU.mult), reads=[PO[e_], sgT], writes=hs)
                units.append([s0, s1a, s1b, s2a, s2b, s3])
        pipeline(units, 6, order=[1, 2, 3, 4, 5, 0], skews=[0, 1, 1, 2, 2, 3])

    def attention_sample(j0, n):
        WH = 8 * n
        v = phase_views("att")
        e32 = [v["e32_0"], v["e32_1"]]
        spb = [v["spb_0"], v["spb_1"], v["spb_2"]]
        PTb = [v["PT_0"], v["PT_1"], v["PT_2"]]
        SS = v["SS_0"]
        hs = hnT_s[0:1]
        units = []

        def nkeys(kb):
            return n if kb >= j0 else 128

        def qk(E, KT, nk, kcol, qbase):
            for h in range(8):
                S.op("pe", lambda e, h=h: e.matmul(out=E.ap[0:nk, n * h:n * (h + 1)], lhsT=KT.ap[:, h // 2, kcol:kcol + nk], rhs=QT.ap[:, qbase + h, 0:n],
                                                   start=(h == 0), stop=True, skip_group_check=True), reads=[KT, QT], writes=[E])

        def gate(POb, chunk0, src_fn):
            for e_ in range(2):
                S.op("dve", lambda e, e_=e_: e.tensor_tensor(out=hnT.ap[64 * e_:64 * e_ + 64, chunk0:chunk0 + 4, 0:n],
                                                            in0=src_fn(e_).rearrange("p (a b c) -> p a b c", a=4, b=2)[:, :, e_, :],
                                                            in1=sgT.ap[64 * e_:64 * e_ + 64, chunk0:chunk0 + 4, 0:n], op=ALU.mult), reads=[POb, sgT], writes=hs)

        PO, PDEN = BK[4], BK[6]
        kbs = list(range(4, j0 + 1))
        for ui, kb in enumerate(kbs):
            nk = nkeys(kb)
            d = j0 - kb
            slot = kb % 8
            u = rot["u"]
            rot["u"] += 1
            E = BK[u % 4]
            P = PTb[u % 3]
            first = ui == 0
            last = ui == len(kbs) - 1

            def s0(E=E, nk=nk, slot=slot, d=d):
                qk(E, KTA, nk, slot * 128, 0)
                if d in (0, 1):
                    for h in range(8):
                        S.op("pe", lambda e, h=h: e.matmul(out=E.ap[0:nk, n * h:n * (h + 1)], lhsT=ident.ap[0:nk, 0:nk], rhs=biasT.ap[0:nk, h, d, 0:n],
                                                           start=False, stop=True, skip_group_check=True), reads=[ident, biasT], writes=[E])

            def s1(E=E, P=P, nk=nk):
                S.op("act", lambda e: e.activation(out=P.ap[0:nk, 0:WH], in_=E.ap[0:nk, 0:WH], func=AF.Exp), reads=[E], writes=[P])

            def s2(P=P, nk=nk, slot=slot, first=first):
                for h in range(8):
                    S.op("pe", lambda e, h=h: e.matmul(out=PO.ap[:, n * h:n * (h + 1)], lhsT=VA.ap[0:nk, slot, (h // 2) * 128:(h // 2 + 1) * 128],
                                                       rhs=P.ap[0:nk, n * h:n * (h + 1)], start=(first and h == 0), stop=True, skip_group_check=True),
                         reads=[VA, P], writes=[PO])
                for h in range(8):
                    for e_ in range(2):
                        S.op("pe", lambda e, h=h, e_=e_: e.matmul(out=PDEN.ap[:, n * h:n * (h + 1)], lhsT=onesLR.ap[0:nk, e_, :], rhs=P.ap[0:nk, n * h:n * (h + 1)],
                                                                  start=(first and h == 0 and e_ == 0), stop=True, skip_group_check=True),
                             reads=[onesLR, P], writes=[PDEN])

            s3 = None
            if last:
                def s3():
                    R = e32[0]
                    S.op("dve", lambda e: e.reciprocal(out=R.ap[:, 0:WH], in_=PDEN.ap[:, 0:WH]), reads=[PDEN], writes=[R])
                    S.op("dve", lambda e: e.tensor_tensor(out=R.ap[:, 0:WH], in0=PO.ap[:, 0:WH], in1=R.ap[:, 0:WH], op=ALU.mult), reads=[PO, R], writes=[R])
                    gate(R, 0, lambda e_: R.ap[64 * e_:64 * e_ + 64, 0:WH])
            units.append([s0, s1, None, s2, None, s3])

        POB = BK[5]
        kbs = list(range(j0, -1, -1))
        for ui, kb in enumerate(kbs):
            nk = nkeys(kb)
            diag = kb >= j0
            u = rot["u"]
            rot["u"] += 1
            E = BK[u % 4]
            P = PTb[u % 3]
            sp = spb[u % 3]
            ee = e32[u % 2]
            first = ui == 0
            last = ui == len(kbs) - 1

            def s0(E=E, nk=nk, kb=kb):
                qk(E, KTB, nk, kb * 128, 8)

            def s1a(E=E, ee=ee, nk=nk):
                S.op("act", lambda e: e.activation(out=ee.ap[0:nk, 0:WH], in_=E.ap[0:nk, 0:WH], func=AF.Exp), reads=[E], writes=[ee])

            def s1b(ee=ee, sp=sp, nk=nk, diag=diag):
                S.op("act", lambda e: e.activation(out=sp.ap[0:nk, 0:WH], in_=ee.ap[0:nk, 0:WH], func=AF.Ln, bias=1.0), reads=[ee], writes=[sp])
                if diag:
                    S.op("dve", lambda e: e.tensor_tensor(out=sp.ap[0:nk, 0:WH], in0=sp.ap[0:nk, 0:WH], in1=mdiag8.ap[0:nk, 0:WH], op=ALU.mult),
                         reads=[sp, mdiag8], writes=[sp])

            def s2a(E=E, sp=sp, nk=nk, kb=kb, first=first):
                S.op("pe", lambda e: e.matmul(out=E.ap[0:nk, 0:WH], lhsT=triM.ap[0:nk, 0:nk], rhs=sp.ap[0:nk, 0:WH], start=False, stop=True,
                                              skip_group_check=True), reads=[triM, sp], writes=[E])
                if not first:
                    S.op("pe", lambda e: e.matmul(out=E.ap[0:nk, 0:WH], lhsT=nones.ap[:, 0:nk], rhs=SS.ap[:, 0:WH], start=False, stop=True,
                                                  skip_group_check=True), reads=[nones, SS], writes=[E])
                if kb > 0:
                    if first:
                        S.op("dve", lambda e: e.memset(SS.ap[:, 0:WH], 0.0), writes=[SS])
                    S.op("dve", lambda e: e.tensor_tensor(out=SS.ap[0:nk, 0:WH], in0=SS.ap[0:nk, 0:WH], in1=sp.ap[0:nk, 0:WH], op=ALU.add),
                         reads=[SS, sp], writes=[SS])

            def s2b(E=E, P=P, nk=nk, diag=diag):
                S.op("act", lambda e: e.activation(out=P.ap[0:nk, 0:WH], in_=E.ap[0:nk, 0:WH], func=AF.Exp), reads=[E], writes=[P])
                if diag:
                    S.op("dve", lambda e: e.tensor_tensor(out=P.ap[0:nk, 0:WH], in0=P.ap[0:nk, 0:WH], in1=mdiag8.ap[0:nk, 0:WH], op=ALU.mult),
                         reads=[P, mdiag8], writes=[P])

            def s3(P=P, nk=nk, kb=kb, first=first, last=last):
                for h in range(8):
                    S.op("pe", lambda e, h=h: e.matmul(out=POB.ap[:, n * h:n * (h + 1)], lhsT=VB.ap[0:nk, kb, (h // 2) * 128:(h // 2 + 1) * 128],
                                                       rhs=P.ap[0:nk, n * h:n * (h + 1)], start=(first and h == 0), stop=True, skip_group_check=True),
                         reads=[VB, P], writes=[POB])
                if last:
                    gate(POB, 4, lambda e_: POB.ap[64 * e_:64 * e_ + 64, 0:WH])
            units.append([s0, s1a, s1b, s2a, s2b, s3])
        pipeline(units, 6, order=[1, 4, 2, 3, 5, 0], skews=[0, 1, 1, 2, 3, 4])

    def phase3(subs, hsrc, hdst, psrc):
        nsub = len(subs)
        v = phase_views("p3")
        hfs = [hf, v["hfB"]]
        sig = v["sig"]
        hTs = [v["hT_0"], v["hT_1"]]
        pfs = [v["pf_0"], v["pf_1"]]
        pb = v["pb"]
        pTs = [v["pT_0"], v["pT_1"]]
        units = []
        for i, n in enumerate(subs):
            h_ = hfs[i % 2]
            pf = pfs[i % 2]
            hT = hTs[i % 2]
            pT = pTs[i % 2]
            sd = st[i % 2]
            yb = [BK[0], BK[1]] if i % 2 == 0 else [BK[2], BK[3]]
            gbs = [(BK[4], BK[5]), (BK[6], BK[7])]

            def p0(i=i, n=n, yb=yb):
                for hlf in range(2):
                    for kc in range(8):
                        S.op("pe", lambda e, kc=kc, hlf=hlf: e.matmul(out=yb[hlf].ap[0:n, :], lhsT=hnT.ap[:, kc, 128 * i:128 * i + n],
                                                                      rhs=Wout.ap[:, kc, hlf * 512:(hlf + 1) * 512], start=(kc == 0), stop=(kc == 7)),
                             reads=[hnT_s[i], Wout], writes=[yb[hlf]])

            def ld(i=i, n=n, h_=h_, pf=pf):
                S.dma("sp", h_.ap[0:n], hsrc(i), writes=[h_])
                S.dma("sp", pf.ap[0:n], psrc(i), writes=[pf])

            def p1a(i=i, n=n, h_=h_, pf=pf, sd=sd, yb=yb):
                for hlf in range(2):
                    S.op("act", lambda e, hlf=hlf: e.activation(out=kbf.ap[0:n, :], in_=yb[hlf].ap[0:n, :], func=AF.Square,
                                                                accum_out=sd["ss"].ap[0:n, hlf:hlf + 1]), reads=[yb[hlf]], writes=[kbf, sd["ss"]])
                S.op("dve", lambda e: e.tensor_tensor(out=sd["s1"].ap[0:n, 0:1], in0=sd["ss"].ap[0:n, 0:1], in1=sd["ss"].ap[0:n, 1:2], op=ALU.add),
                     reads=[sd["ss"]], writes=[sd["s1"]])
                rstd_from(sd, "s1", n)
                for hlf in range(2):
                    S.op("dve", lambda e, hlf=hlf: e.scalar_tensor_tensor(out=yb[hlf].ap[0:n, :], in0=yb[hlf].ap[0:n, :], scalar=sd["r"].ap[0:n, 0:1],
                                                                         in1=gpost.ap[0:n, hlf * 512:(hlf + 1) * 512], op0=ALU.mult, op1=ALU.mult),
                         reads=[yb[hlf], sd["r"], gpost], writes=[yb[hlf]])
                    S.op("dve", lambda e, hlf=hlf: e.tensor_tensor(out=h_.ap[0:n, hlf * 512:(hlf + 1) * 512], in0=h_.ap[0:n, hlf * 512:(hlf + 1) * 512],
                                                                  in1=yb[hlf].ap[0:n, :], op=ALU.add), reads=[h_, yb[hlf]], writes=[h_])
                S.op("dve", lambda e: e.tensor_copy(out=pb.ap[0:n], in_=pf.ap[0:n]), reads=[pf], writes=[pb])

            def p1c(n=n, h_=h_):
                S.op("act", lambda e: e.activation(out=hn.ap[0:n], in_=h_.ap[0:n], func=AF.Copy), reads=[h_], writes=[hn])

            def p1b(n=n, hT=hT, pT=pT, yb=yb):
                transpose_to(hn, n, 8, lambda: hT.ap[:, :, 0:n], [hT], ptr=yb[0])
                transpose_to(pb, n, 2, lambda: pT.ap[:, :, 0:n], [pT], ptr=yb[1])

            def p3pe(n=n, hT=hT, pT=pT, gbs=gbs):
                for hlf in range(2):
                    gb, pk = gbs[hlf]
                    for kc in range(8):
                        S.op("pe", lambda e, kc=kc, hlf=hlf, gb=gb: e.matmul(out=gb.ap[0:n, :], lhsT=hT.ap[:, kc, 0:n], rhs=Wgate.ap[:, kc, hlf * 512:(hlf + 1) * 512],
                                                                             start=(kc == 0), stop=(kc == 7)), reads=[hT, Wgate], writes=[gb])
                    for kc in range(2):
                        S.op("pe", lambda e, kc=kc, hlf=hlf, pk=pk: e.matmul(out=pk.ap[0:n, :], lhsT=pT.ap[:, kc, 0:n], rhs=Wple.ap[:, kc, hlf * 512:(hlf + 1) * 512],
                                                                             start=(kc == 0), stop=(kc == 1)), reads=[pT, Wple], writes=[pk])

            def p3ev(i=i, n=n, h_=h_, gbs=gbs):
                for hlf in range(2):
                    gb, pk = gbs[hlf]
                    S.op("act", lambda e, gb=gb: e.activation(out=sig.ap[0:n, :], in_=gb.ap[0:n, :], func=AF.Sigmoid), reads=[gb], writes=[sig])
                    S.op("dve", lambda e, pk=pk: e.tensor_tensor(out=sig.ap[0:n, :], in0=pk.ap[0:n, :], in1=sig.ap[0:n, :], op=ALU.mult),
                         reads=[pk, sig], writes=[sig])
                    S.op("pool", lambda e, hlf=hlf: e.tensor_tensor(out=h_.ap[0:n, hlf * 512:(hlf + 1) * 512], in0=h_.ap[0:n, hlf * 512:(hlf + 1) * 512],
                                                                   in1=sig.ap[0:n, :], op=ALU.add), reads=[h_, sig], writes=[h_])
                S.dma("sp", hdst(i), h_.ap[0:n], reads=[h_])

            units.append([p0, ld, p1a, p1c, p1b, p3pe, p3ev])
        pipeline(units, 7, order=[2, 5, 0, 3, 6, 4, 1], skews=[0, 0, 1, 1, 1, 2, 2])

    def supertile(l, j0, subs, hsrc, hdst, psrc, kvout, after_p1=None):
        phase1(l, j0, subs, hsrc, kvout)
        if after_p1 is not None:
            after_p1()
        if state["sample"]:
            attention_sample(j0, subs[0])
        else:
            attention(j0, subs)
        phase3(subs, hsrc, hdst, psrc)

    def load_cache(l, s):
        v = phase_views("p1")
        kbfs = [kbf, v["kbf2"]]
        stgs = list(stage)
        for nm in ("h1", "h2", "h3"):
            for hh in range(2):
                stgs.append(Buf(nm + "s%d" % hh, v[nm].ap[:, hh * 512:(hh + 1) * 512]))
        S.alias(stgs[2:], state["xbufs"])
        state["xbufs"] = state["xbufs"] + stgs[2:]
        k = 0
        jobs = []
        for (ck, cv, KT, VV, blks, off) in [(cache_b_k, cache_b_v, KTB, VB, range(8), 0), (cache_a_k, cache_a_v, KTA, VA, range(4, 8), 4)]:
            for blk in blks:
                jobs.append((ck, KT, blk, off, True))
                jobs.append((cv, VV, blk, off, False))
        AHEAD = 6
        loaded = []
        for ji in range(len(jobs) + AHEAD):
            if ji < len(jobs):
                src, dstb, blk, off, isk = jobs[ji]
                stg = stgs[ji % len(stgs)]
                S.dma("sp", stg.ap[:], src[l, s, (blk - off) * 128:(blk - off + 1) * 128, :], writes=[stg])
                loaded.append(stg)
            jj = ji - AHEAD
            if jj >= 0:
                src, dstb, blk, off, isk = jobs[jj]
                stg = loaded[jj]
                if isk:
                    kb2 = kbfs[k % 2]
                    k += 1
                    S.op("dve", lambda e, stg=stg, kb2=kb2: e.tensor_copy(out=kb2.ap[:], in_=stg.ap[:]), reads=[stg], writes=[kb2])
                    transpose_to(kb2, 128, 4, lambda blk=blk, KT=dstb: KT.ap[:, :, blk * 128:(blk + 1) * 128], [dstb])
                else:
                    S.op("act", lambda e, stg=stg, blk=blk, VV=dstb: e.activation(out=VV.ap[:, blk, :], in_=stg.ap[:], func=AF.Copy), reads=[stg], writes=[dstb])

    load_win(0)
    for l in range(DEPTH):
        load_rest(l)
        src_p = x_prompt if l == 0 else h1p
        dst_p = h1p if l == 0 else y_prompt
        src_s = x_sample if l == 0 else h1s
        dst_s = h1s if l == 0 else y_sample
        state["sample"] = True
        for s in range(NS):
            load_cache(l, s)
            supertile(l, 8, [DSEQ],
                      lambda i, s=s: src_s[s, :, :],
                      lambda i, s=s: dst_s[s, :, :],
                      lambda i, s=s, l=l: p_sample[l, s, :, :],
                      lambda i, s=s, l=l: {"ka": sak[l, s, :, :], "va": sav[l, s, :, :], "kb": sbk[l, s, :, :], "vb": sbv[l, s, :, :]})
        state["sample"] = False
        for s in range(NB):
            for T in range(4):
                def kvout(i, T=T, s=s, l=l):
                    t0 = 512 * T + 128 * i
                    dct = {"kb": pbk[l, s, t0:t0 + 128, :], "vb": pbv[l, s, t0:t0 + 128, :], "ka": None, "va": None}
                    if t0 >= SEQ - 512:
                        dct["ka"] = pak[l, s, t0 - (SEQ - 512):t0 - (SEQ - 512) + 128, :]
                        dct["va"] = pav[l, s, t0 - (SEQ - 512):t0 - (SEQ - 512) + 128, :]
                    return dct
                last_p1 = (s == NB - 1 and T == 3 and l + 1 < DEPTH)
                supertile(l, 4 * T, [128] * 4,
                          lambda i, T=T, s=s: src_p[s, 512 * T + 128 * i:512 * T + 128 * (i + 1), :],
                          lambda i, T=T, s=s: dst_p[s, 512 * T + 128 * i:512 * T + 128 * (i + 1), :],
                          lambda i, T=T, s=s, l=l: p_prompt[l, s, 512 * T + 128 * i:512 * T + 128 * (i + 1), :],
                          kvout, after_p1=((lambda l=l: load_win(l + 1)) if last_p1 else None))
    S.finish()
    return nc, es


_CACHE = {}


def kernel(x_prompt, x_sample, p_prompt, p_sample, cache_a_k, cache_a_v, cache_b_k, cache_b_v,
           g_pre, w_in, rel_bias, w_out, g_post, w_ple, w_ple_gate):
    f = lambda a: np.ascontiguousarray(np.asarray(a, dtype=np.float32))
    x_prompt, x_sample, p_prompt, p_sample = f(x_prompt), f(x_sample), f(p_prompt), f(p_sample)
    cache_a_k, cache_a_v, cache_b_k, cache_b_v = f(cache_a_k), f(cache_a_v), f(cache_b_k), f(cache_b_v)
    rel_bias = f(rel_bias)
    s_ = np.arange(128)[:, None]
    t_ = np.arange(128)[None, :]
    idx0 = np.clip(s_ - t_, -128, 128) + 128
    idx1 = np.clip(s_ - t_ - 128, -128, 128) + 128
    bt = np.stack([rel_bias[:, :, idx0], rel_bias[:, :, idx1]], axis=2)
    bias_t = np.ascontiguousarray(bt.transpose(0, 3, 1, 2, 4)).reshape(DEPTH, 128, 8 * 2 * 128)
    bias_c = np.ascontiguousarray(rel_bias[:, :, 0])

    if "nc" not in _CACHE:
        _CACHE["nc"] = build_program()
    nc, es = _CACHE["nc"]
    in_maps = []
    for c in range(NCORES):
        b0, b1 = NB * c, NB * (c + 1)
        in_maps.append({
            "x_prompt": x_prompt[b0:b1], "x_sample": x_sample[b0:b1],
            "p_prompt": np.ascontiguousarray(p_prompt[:, b0:b1]), "p_sample": np.ascontiguousarray(p_sample[:, b0:b1]),
            "cache_a_k": np.ascontiguousarray(cache_a_k[:, b0:b1]).reshape(DEPTH, NS, 512, 512),
            "cache_a_v": np.ascontiguousarray(cache_a_v[:, b0:b1]).reshape(DEPTH, NS, 512, 512),
            "cache_b_k": np.ascontiguousarray(cache_b_k[:, b0:b1]).reshape(DEPTH, NS, PAST, 512),
            "cache_b_v": np.ascontiguousarray(cache_b_v[:, b0:b1]).reshape(DEPTH, NS, PAST, 512),
            "g_pre": f(g_pre), "g_post": f(g_post), "w_in": f(w_in), "w_out": f(w_out), "w_ple": f(w_ple),
            "w_ple_gate": f(w_ple_gate), "bias_t": bias_t, "bias_c": bias_c,
        })
    res = run_bass_kernel_spmd(nc, in_maps, core_ids=list(range(NCORES)))
    R = res.results
    cat0 = lambda k: np.concatenate([r[k] for r in R], axis=0)
    cat1 = lambda k, shp: np.concatenate([r[k] for r in R], axis=1).reshape(shp)
    B = NB * NCORES
    return (cat0("y_prompt"), cat0("y_sample"),
            cat1("pak", (DEPTH, B, 512, 8, 64)), cat1("pav", (DEPTH, B, 512, 8, 64)),
            cat1("pbk", (DEPTH, B, SEQ, 8, 64)), cat1("pbv", (DEPTH, B, SEQ, 8, 64)),
            cat1("sak", (DEPTH, B, DSEQ, 8, 64)), cat1("sav", (DEPTH, B, DSEQ, 8, 64)),
            cat1("sbk", (DEPTH, B, DSEQ, 8, 64)), cat1("sbv", (DEPTH, B, DSEQ, 8, 64)))
```

```python
import numpy as np
from contextlib import ExitStack
import concourse.bass as bass
import concourse.mybir as mybir
from concourse.bass_utils import run_bass_kernel_spmd

F32 = mybir.dt.float32
BF16 = mybir.dt.bfloat16
AF = mybir.ActivationFunctionType
ALU = mybir.AluOpType

NCORES = 8
DEPTH = 2
D = 1024
SEQ = 2048
DSEQ = 32
PAST = 1024
NB = 2
NS = 2
EPS = 1e-6
NEG = -30000.0
STRICT = True
QA, KA, VA_, GA, QB, KB, VB_, GB = [512 * i for i in range(8)]


class Buf:
    __slots__ = ("name", "ap", "last_w", "readers", "dma_readers", "bf", "psum")

    def __init__(self, name, ap):
        self.name = name
        self.ap = ap
        self.last_w = None
        self.readers = {}
        self.dma_readers = []
        self.psum = False


class Op:
    __slots__ = ("eng", "fn", "deps", "idx", "signal", "val", "is_dma", "dsem", "dval", "ringwait")

    def __init__(self, eng, fn, is_dma):
        self.eng = eng
        self.fn = fn
        self.deps = []
        self.signal = False
        self.val = 0
        self.is_dma = is_dma
        self.dsem = None
        self.dval = 0
        self.ringwait = None


class Sched:
    RING = {"sp": 40, "pool": 24}

    def __init__(self, nc, es):
        self.nc = nc
        self.es = es
        self.engs = {"pe": nc.tensor, "act": nc.scalar, "dve": nc.vector, "pool": nc.gpsimd, "sp": nc.sync}
        self.ops = {k: [] for k in self.engs}
        self.sem = {k: es.enter_context(nc.semaphore("s_" + k)) for k in self.engs}
        self.ring = {q: [es.enter_context(nc.semaphore("r_%s%d" % (q, i))) for i in range(n)] for q, n in self.RING.items()}
        self.ndma = {q: 0 for q in self.RING}

    def sb(self, name, shape, dtype):
        return Buf(name, self.nc.alloc_sbuf_tensor(name, list(shape), dtype))

    def ps(self, name, shape, dtype):
        b = Buf(name, self.nc.alloc_psum_tensor(name, list(shape), dtype))
        b.psum = True
        return b

    def _adddeps(self, op, reads, writes):
        deps = []
        for b in reads:
            w = b.last_w
            if w is not None:
                if w.eng == op.eng and not w.is_dma and not op.is_dma:
                    if op.eng != "pe":
                        deps.append(w)
                else:
                    deps.append(w)
            if b.psum:
                for e, r in b.readers.items():
                    if e != op.eng:
                        deps.append(r)
        strict = STRICT and op.eng != "pe"
        for b in writes:
            w = b.last_w
            if w is not None and (w.is_dma or op.is_dma or w.eng != op.eng or strict):
                deps.append(w)
            for e, r in b.readers.items():
                if r.is_dma or op.is_dma or e != op.eng or strict:
                    deps.append(r)
            deps.extend(b.dma_readers)
        seen = set()
        for d in deps:
            if id(d) not in seen and d is not op:
                seen.add(id(d))
                op.deps.append(d)
                if not d.is_dma:
                    d.signal = True
        for b in writes:
            b.last_w = op
            b.readers = {}
            b.dma_readers = []
        for b in reads:
            if b.last_w is op:
                continue
            if op.is_dma:
                b.dma_readers.append(op)
            else:
                b.readers[op.eng] = op

    def op(self, eng, fn, reads=(), writes=()):
        o = Op(eng, fn, False)
        self.ops[eng].append(o)
        self._adddeps(o, reads, writes)
        return o

    def dma(self, q, out, in_, reads=(), writes=(), **kw):
        def fn(e, out=out, in_=in_, kw=kw):
            return e.dma_start(out=out, in_=in_, **kw)

        o = Op(q, fn, True)
        self.ops[q].append(o)
        n = self.ndma[q]
        self.ndma[q] = n + 1
        R = len(self.ring[q])
        o.dsem = self.ring[q][n % R]
        o.dval = 16 * (n // R + 1)
        if n >= R:
            o.ringwait = (o.dsem, 16 * (n // R))
        self._adddeps(o, reads, writes)
        return o

    def alias(self, newbufs, oldbufs):
        for nb in newbufs:
            for ob in oldbufs:
                w = ob.last_w
                if w is not None:
                    if w.is_dma:
                        nb.dma_readers.append(w)
                    else:
                        nb.readers["w_" + ob.name + w.eng] = w
                for e, r in ob.readers.items():
                    nb.readers["r_" + ob.name + str(e)] = r
                nb.dma_readers.extend(ob.dma_readers)

    def finish(self):
        for k, lst in self.ops.items():
            c = 0
            for o in lst:
                if o.signal:
                    c += 1
                    o.val = c
        for k, lst in self.ops.items():
            e = self.engs[k]
            waited = {}

            def wait(sem, val):
                key = id(sem)
                if waited.get(key, 0) >= val:
                    return
                waited[key] = val
                e.wait_ge(sem, val)

            for o in lst:
                if o.ringwait is not None:
                    wait(*o.ringwait)
                for d in o.deps:
                    if d.is_dma:
                        wait(d.dsem, d.dval)
                    else:
                        wait(self.sem[d.eng], d.val)
                inst = o.fn(e)
                if o.is_dma:
                    inst.then_inc(o.dsem, 16)
                elif o.signal:
                    inst.then_inc(self.sem[k], 1)
            if k in self.ring:
                n = self.ndma[k]
                R = len(self.ring[k])
                for i in range(min(n, R)):
                    wait(self.ring[k][i], 16 * ((n - 1 - i) // R + 1))


def build_program():
    nc = bass.Bass("TRN2", target_bir_lowering=False, dynamic_dma_scratch_size=4096)
    es = ExitStack()
    S = Sched(nc, es)

    def din(name, shape):
        return nc.dram_tensor(name, list(shape), F32, kind="ExternalInput").ap()

    def dout(name, shape):
        return nc.dram_tensor(name, list(shape), F32, kind="ExternalOutput").ap()

    x_prompt = din("x_prompt", [NB, SEQ, D])
    x_sample = din("x_sample", [NS, DSEQ, D])
    p_prompt = din("p_prompt", [DEPTH, NB, SEQ, 256])
    p_sample = din("p_sample", [DEPTH, NS, DSEQ, 256])
    cache_a_k = din("cache_a_k", [DEPTH, NS, 512, 512])
    cache_a_v = din("cache_a_v", [DEPTH, NS, 512, 512])
    cache_b_k = din("cache_b_k", [DEPTH, NS, PAST, 512])
    cache_b_v = din("cache_b_v", [DEPTH, NS, PAST, 512])
    g_pre = din("g_pre", [DEPTH, D])
    g_post = din("g_post", [DEPTH, D])
    w_in = din("w_in", [DEPTH, D, 4096])
    w_out = din("w_out", [DEPTH, D, D])
    w_ple = din("w_ple", [DEPTH, 256, D])
    w_gate = din("w_ple_gate", [DEPTH, D, D])
    bias_t = din("bias_t", [DEPTH, 128, 8 * 2 * 128])
    bias_c = din("bias_c", [DEPTH, 8])

    y_prompt = dout("y_prompt", [NB, SEQ, D])
    y_sample = dout("y_sample", [NS, DSEQ, D])
    pak = dout("pak", [DEPTH, NB, 512, 512])
    pav = dout("pav", [DEPTH, NB, 512, 512])
    pbk = dout("pbk", [DEPTH, NB, SEQ, 512])
    pbv = dout("pbv", [DEPTH, NB, SEQ, 512])
    sak = dout("sak", [DEPTH, NS, DSEQ, 512])
    sav = dout("sav", [DEPTH, NS, DSEQ, 512])
    sbk = dout("sbk", [DEPTH, NS, DSEQ, 512])
    sbv = dout("sbv", [DEPTH, NS, DSEQ, 512])
    h1p = nc.dram_tensor("h1p", [NB, SEQ, D], F32).ap()
    h1s = nc.dram_tensor("h1s", [NS, DSEQ, D], F32).ap()

    Win = S.sb("Win", [128, 8, 4096], BF16)
    Wout = S.sb("Wout", [128, 8, 1024], BF16)
    Wgate = S.sb("Wgate", [128, 8, 1024], BF16)
    Wple = S.sb("Wple", [128, 2, 1024], BF16)
    KTB = S.sb("KTB", [128, 4, 2048], BF16)
    VB = S.sb("VB", [128, 16, 512], BF16)
    KTA = S.sb("KTA", [128, 4, 1024], BF16)
    VA = S.sb("VA", [128, 8, 512], BF16)
    hnT = S.sb("hnT", [128, 8, 512], BF16)
    QT = S.sb("QT", [128, 16, 512], BF16)
    sgT = S.sb("sgT", [128, 8, 512], BF16)
    gpre = S.sb("gpre", [128, 1024], F32)
    gpost = S.sb("gpost", [128, 1024], F32)
    biasT = S.sb("biasT", [128, 8, 2, 128], BF16)
    bias4 = S.sb("bias4", [128, 128], BF16)
    cvec = S.sb("cvec", [128, 8], F32)
    ident = S.sb("ident", [128, 128], BF16)
    triM = S.sb("triM", [128, 128], BF16)
    nones = S.sb("nones", [128, 128], BF16)
    onesLR = S.sb("onesLR", [128, 2, 128], BF16)
    mdiag = S.sb("mdiag", [128, 128], BF16)
    hf = S.sb("hf", [128, 1024], F32)
    hn = S.sb("hn", [128, 1024], BF16)
    stage = [S.sb("stage%d" % i, [128, 512], F32) for i in range(2)]
    kbf = S.sb("kbf", [128, 512], BF16)
    st_ss = S.sb("st_ss", [128, 2], F32)
    st_s1 = S.sb("st_s1", [128, 1], F32)
    st_ln = S.sb("st_ln", [128, 1], F32)
    st_r = S.sb("st_r", [128, 1], F32)
    XB = 14336
    X = nc.alloc_sbuf_tensor("X", [128, XB], mybir.dt.uint8)
    LAYOUTS = {
        "p1": [("h1", [128, 1024], F32), ("h2", [128, 1024], F32), ("h3", [128, 1024], F32), ("kbf2", [128, 512], BF16)],
        "att": [("e32_0", [128, 512], F32), ("e32_1", [128, 512], F32), ("spb_0", [128, 512], BF16), ("spb_1", [128, 512], BF16),
                ("spb_2", [128, 512], BF16), ("spb_3", [128, 512], BF16), ("PT_0", [128, 512], BF16), ("PT_1", [128, 512], BF16),
                ("PT_2", [128, 512], BF16), ("PT_3", [128, 512], BF16), ("SS_0", [128, 512], BF16), ("SS_1", [128, 512], BF16)],
        "p3": [("hfB", [128, 1024], F32), ("sig", [128, 512], F32), ("hT_0", [128, 8, 128], BF16), ("hT_1", [128, 8, 128], BF16),
               ("pf_0", [128, 256], F32), ("pf_1", [128, 256], F32), ("pb", [128, 256], BF16), ("pT_0", [128, 2, 128], BF16), ("pT_1", [128, 2, 128], BF16)],
    }
    VIEWS = {}
    for ph, lst in LAYOUTS.items():
        o = 0
        VIEWS[ph] = {}
        for nm, shp, dt in lst:
            n = int(np.prod(shp[1:])) * (4 if dt == F32 else 2)
            ap = X[:, o:o + n].bitcast(dt)
            if len(shp) == 3:
                ap = ap.rearrange("p (a b) -> p a b", a=shp[1])
            VIEWS[ph][nm] = ap
            VIEWS[ph][nm + "@off"] = o
            o += n
        assert o <= XB, (ph, o)
    state = {"xbufs": [], "sample": False}

    def xpair(ph, nm, dt):
        o = VIEWS[ph][nm + "@off"]
        nbytes = 2 * 512 * (4 if dt == F32 else 2)
        return X[:, o:o + nbytes].bitcast(dt).rearrange("p (a b) -> p a b", a=2)

    def phase_views(ph):
        bufs = {nm: Buf(nm, ap) for nm, ap in VIEWS[ph].items() if not nm.endswith("@off")}
        S.alias(list(bufs.values()), state["xbufs"])
        state["xbufs"] = list(bufs.values())
        return bufs

    BIG = nc.alloc_psum_tensor("BIG", [128, 8, 512], F32)
    BK = []
    for i in range(8):
        bk = Buf("BK%d" % i, BIG[:, i, :])
        bk.psum = True
        bk.bf = BIG[:, i, :].bitcast(BF16)
        BK.append(bk)
    EB = BK[0:3]
    PO2 = [BK[3], BK[4]]
    PDEN2 = [BK[5], BK[6]]
    dense_banks = BK[0:6]
    PTR = [BK[6], BK[7]]
    rot = {"d": 0, "s": 0, "t": 0, "u": 0}

    def next_bank():
        b = dense_banks[rot["d"] % len(dense_banks)]
        rot["d"] += 1
        return b

    def next_stage():
        b = stage[rot["s"] % 2]
        rot["s"] += 1
        return b

    def next_ptr():
        b = PTR[rot["t"] % 2]
        rot["t"] += 1
        return b

    hnT_s = [Buf("hnT_s%d" % i, hnT.ap) for i in range(4)]
    st = [{k: S.sb("st_%s%d" % (k, i), [128, 2], F32) for k in ("ss", "s1", "ln", "r")} for i in range(2)]

    S.op("pool", lambda e: e.memset(ident.ap[:], 0.0), writes=[ident])
    S.op("pool", lambda e: e.affine_select(out=ident.ap[:], in_=ident.ap[:], pattern=[[-1, 128]], compare_op=ALU.not_equal,
                                           fill=1.0, base=0, channel_multiplier=1), reads=[ident], writes=[ident])
    S.op("pool", lambda e: e.memset(triM.ap[:], -1.0), writes=[triM])
    S.op("pool", lambda e: e.affine_select(out=triM.ap[:], in_=triM.ap[:], pattern=[[-1, 128]], compare_op=ALU.is_ge,
                                           fill=0.0, base=0, channel_multiplier=1), reads=[triM], writes=[triM])
    S.op("pool", lambda e: e.memset(mdiag.ap[:], 1.0), writes=[mdiag])
    S.op("pool", lambda e: e.affine_select(out=mdiag.ap[:], in_=mdiag.ap[:], pattern=[[1, 128]], compare_op=ALU.is_ge,
                                           fill=0.0, base=-1, channel_multiplier=-1), reads=[mdiag], writes=[mdiag])
    mdiag8 = S.sb("mdiag8", [128, 8 * DSEQ], BF16)
    for h in range(8):
        S.op("pool", lambda e, h=h: e.tensor_copy(out=mdiag8.ap[:, DSEQ * h:DSEQ * (h + 1)], in_=mdiag.ap[:, 0:DSEQ]), reads=[mdiag], writes=[mdiag8])
    S.op("pool", lambda e: e.memset(nones.ap[:], -1.0), writes=[nones])
    S.op("pool", lambda e: e.memset(onesLR.ap[:], 0.0), writes=[onesLR])
    S.op("pool", lambda e: e.memset(onesLR.ap[:, 0, 0:64], 1.0), reads=[onesLR], writes=[onesLR])
    S.op("pool", lambda e: e.memset(onesLR.ap[:, 1, 64:128], 1.0), reads=[onesLR], writes=[onesLR])
    S.op("pool", lambda e: e.memset(QT.ap[:], 0.0), writes=[QT])
    S.op("pool", lambda e: e.memset(bias4.ap[:], 0.0), writes=[bias4])
    S.op("pool", lambda e: e.memset(bias4.ap[0:64, 64:128], NEG), reads=[bias4], writes=[bias4])

    def load_win(l):
        for kc in range(8):
            for hlf in range(2):
                S.dma("pool", Win.ap[:, kc, hlf * 2048:(hlf + 1) * 2048], w_in[l, kc * 128:(kc + 1) * 128, hlf * 2048:(hlf + 1) * 2048], writes=[Win])

    def load_rest(l):
        for kc in range(8):
            S.dma("pool", Wout.ap[:, kc, :], w_out[l, kc * 128:(kc + 1) * 128, :], writes=[Wout])
            S.dma("pool", Wgate.ap[:, kc, :], w_gate[l, kc * 128:(kc + 1) * 128, :], writes=[Wgate])
        for kc in range(2):
            S.dma("pool", Wple.ap[:, kc, :], w_ple[l, kc * 128:(kc + 1) * 128, :], writes=[Wple])
        S.dma("sp", gpre.ap[:], g_pre[l:l + 1, :].partition_broadcast(128), writes=[gpre])
        S.dma("sp", gpost.ap[:], g_post[l:l + 1, :].partition_broadcast(128), writes=[gpost])
        S.dma("sp", cvec.ap[:], bias_c[l:l + 1, :].partition_broadcast(128), writes=[cvec])
        v = phase_views("p1")
        tmp = v["h1"]
        for hh in range(2):
            S.dma("sp", tmp.ap[:], bias_t[l, :, hh * 1024:(hh + 1) * 1024], writes=[tmp])
            for h4 in range(4):
                h = hh * 4 + h4
                S.op("dve", lambda e, h=h, h4=h4: e.tensor_scalar(out=biasT.ap[:, h].rearrange("p a b -> p (a b)"), in0=tmp.ap[:, h4 * 256:(h4 + 1) * 256],
                                                               scalar1=cvec.ap[:, h:h + 1], scalar2=None, op0=ALU.subtract), reads=[tmp, cvec], writes=[biasT])
        S.op("dve", lambda e: e.memset(biasT.ap[64:128, :, 0, 0:64], NEG), reads=[biasT], writes=[biasT])

    def transpose_to(src, n, nchunk, dst_fn, dst_bufs, ptr=None):
        if ptr is None:
            ptr = next_ptr()
        for c in range(nchunk):
            S.op("pe", lambda e, c=c, ptr=ptr: e.transpose(out=ptr.bf[:, c * 128:c * 128 + n], in_=src.ap[0:n, c * 128:(c + 1) * 128],
                                                           identity=ident.ap[0:n, 0:n]), reads=[src, ident], writes=[ptr])
        S.op("dve", lambda e, ptr=ptr: e.tensor_copy(out=dst_fn(), in_=ptr.bf[:, 0:nchunk * 128].rearrange("p (c t) -> p c t", c=nchunk)[:, :, 0:n]),
             reads=[ptr], writes=dst_bufs)

    def rstd_from(sd, ssq_key, n):
        S.op("act", lambda e: e.activation(out=sd["ln"].ap[0:n, 0:1], in_=sd[ssq_key].ap[0:n, 0:1], func=AF.Ln, scale=1.0 / D, bias=EPS),
             reads=[sd[ssq_key]], writes=[sd["ln"]])
        S.op("act", lambda e: e.activation(out=sd["r"].ap[0:n, 0:1], in_=sd["ln"].ap[0:n, 0:1], func=AF.Exp, scale=-0.5), reads=[sd["ln"]], writes=[sd["r"]])

    def pipeline(units, nst, order=None, skews=None):
        n = len(units)
        if order is None:
            order = list(range(nst - 1, -1, -1))
        if skews is None:
            skews = list(range(nst))
        for t in range(n + max(skews)):
            for s_ in order:
                u = t - skews[s_]
                if 0 <= u < n and units[u][s_] is not None:
                    units[u][s_]()

    def phase1(l, j0, subs, hsrc, kvout):
        nsub = len(subs)
        W = 128 * (nsub - 1) + subs[-1]
        v = phase_views("p1")
        hb = [hf, v["h1"], v["h2"], v["h3"]]
        hnb = [hn, hn]
        kbfs = [kbf, v["kbf2"]]
        for i, n in enumerate(subs):
            S.dma("sp", hb[i].ap[0:n], hsrc(i), writes=[hb[i]])

        def rmsA(i):
            n = subs[i]
            sd = st[i % 2]
            h_, hn_ = hb[i], hnb[i % 2]
            S.op("act", lambda e: e.activation(out=sgT.ap[0:n, 0:2, :].rearrange("p a b -> p (a b)"), in_=h_.ap[0:n], func=AF.Square,
                                               accum_out=sd["ss"].ap[0:n, 0:1]), reads=[h_], writes=[sgT, sd["ss"]])
            rstd_from(sd, "ss", n)
            S.op("dve", lambda e: e.scalar_tensor_tensor(out=hn_.ap[0:n], in0=h_.ap[0:n], scalar=sd["r"].ap[0:n, 0:1], in1=gpre.ap[0:n],
                                                         op0=ALU.mult, op1=ALU.mult), reads=[h_, sd["r"], gpre], writes=[hn_])

        def rmsT(i):
            n = subs[i]
            transpose_to(hnb[i % 2], n, 8, lambda: hnT.ap[:, :, 128 * i:128 * i + n], [hnT_s[i]])

        def kv(i):
            n = subs[i]
            kb_ = j0 + i
            outs = kvout(i)
            deferred = []
            for gi, (nm, col) in enumerate([("ka", KA), ("va", VA_), ("kb", KB), ("vb", VB_)]):
                bank = next_bank()
                for kc in range(8):
                    S.op("pe", lambda e, kc=kc, col=col, bank=bank: e.matmul(out=bank.ap[0:n, :], lhsT=hnT.ap[:, kc, 128 * i:128 * i + n],
                                                                              rhs=Win.ap[:, kc, col:col + 512], start=(kc == 0), stop=(kc == 7)),
                         reads=[Win, hnT_s[i]], writes=[bank])
                src_b, src_ap = bank, (lambda bank=bank: bank.ap[0:n, :])
                if outs.get(nm) is not None:
                    stg = next_stage()
                    S.op("act", lambda e, bank=bank, stg=stg: e.activation(out=stg.ap[0:n], in_=bank.ap[0:n, :], func=AF.Copy), reads=[bank], writes=[stg])
                    S.dma("sp", outs[nm], stg.ap[0:n], reads=[stg])
                    src_b, src_ap = stg, (lambda stg=stg: stg.ap[0:n])
                if nm == "va":
                    S.op("dve", lambda e, src_ap=src_ap, s=kb_ % 8: e.tensor_copy(out=VA.ap[0:n, s, :], in_=src_ap()), reads=[src_b], writes=[VA])
                elif nm == "vb":
                    S.op("dve", lambda e, src_ap=src_ap, s=kb_: e.tensor_copy(out=VB.ap[0:n, s, :], in_=src_ap()), reads=[src_b], writes=[VB])
                else:
                    kb2 = kbfs[gi // 2]
                    S.op("dve", lambda e, src_ap=src_ap, kb2=kb2: e.tensor_copy(out=kb2.ap[0:n], in_=src_ap()), reads=[src_b], writes=[kb2])
                    if nm == "ka":
                        c0 = (kb_ % 8) * 128
                        deferred.append(lambda kb2=kb2, c0=c0: transpose_to(kb2, n, 4, lambda: KTA.ap[:, :, c0:c0 + n], [KTA]))
                    else:
                        c0 = kb_ * 128
                        deferred.append(lambda kb2=kb2, c0=c0: transpose_to(kb2, n, 4, lambda: KTB.ap[:, :, c0:c0 + n], [KTB]))
            for f_ in deferred:
                f_()

        rmsA(0)
        rmsT(0)
        for i in range(nsub):
            if i + 1 < nsub:
                rmsA(i + 1)
            kv(i)
            if i + 1 < nsub:
                rmsT(i + 1)
        hs = hnT_s[0:nsub]
        for (col, dst, di, fn_, sc) in (
                [(QA + 128 * c, QT, c, AF.Copy, 0.125) for c in range(4)] + [(QB + 128 * c, QT, 4 + c, AF.Copy, 0.125) for c in range(4)] +
                [(GA + 128 * c, sgT, c, AF.Silu, 1.0) for c in range(4)] + [(GB + 128 * c, sgT, 4 + c, AF.Silu, 1.0) for c in range(4)]):
            bank = next_bank()
            for kc in range(8):
                S.op("pe", lambda e, kc=kc, col=col, bank=bank: e.matmul(out=bank.ap[:, 0:W], lhsT=Win.ap[:, kc, col:col + 128], rhs=hnT.ap[:, kc, 0:W],
                                                                          start=(kc == 0), stop=(kc == 7)), reads=[Win] + hs, writes=[bank])
            if fn_ == AF.Copy:
                for e_ in range(2):
                    S.op("dve", lambda e, bank=bank, dst=dst, di=di, sc=sc, e_=e_: e.tensor_scalar(out=dst.ap[64 * e_:64 * e_ + 64, 2 * di + e_, 0:W],
                                                                                               in0=bank.ap[64 * e_:64 * e_ + 64, 0:W], scalar1=sc, scalar2=None, op0=ALU.mult),
                         reads=[bank], writes=[dst])
            else:
                S.op("act", lambda e, bank=bank, dst=dst, di=di, fn_=fn_, sc=sc: e.activation(out=dst.ap[:, di, 0:W], in_=bank.ap[:, 0:W], func=fn_, scale=sc),
                     reads=[bank], writes=[dst])

    def attention(j0, subs):
        nsub = len(subs)
        W = 128 * (nsub - 1) + subs[-1]
        v = phase_views("att")
        e32 = [v["e32_0"], v["e32_1"]]
        PTb = [v["PT_0"], v["PT_1"], v["PT_2"]]
        hs = hnT_s[0:nsub]

        def nkeys(kb):
            return subs[kb - j0] if kb >= j0 else 128

        units = []
        kb_lo = 4 if state["sample"] else max(j0 - 4, 0)
        for hp in range(4):
            PO, PDEN = ([BK[2], BK[3]], BK[4]) if hp % 2 == 0 else ([BK[5], BK[6]], BK[7])
            ulist = []
            for kb in range(kb_lo, j0 + nsub):
                jlo = max(kb, j0)
                jhi = min(kb + 4, j0 + nsub - 1)
                if jlo > jhi:
                    continue
                for h in (2 * hp, 2 * hp + 1):
                    ulist.append((h, kb, jlo, jhi))
            seen_h = set()
            for ui, (h, kb, jlo, jhi) in enumerate(ulist):
                first = h not in seen_h
                seen_h.add(h)
                last = ui == len(ulist) - 1
                po = 64 * (h % 2)
                nk = nkeys(kb)
                c0 = 128 * (jlo - j0)
                c1 = 128 * (jhi - j0) + subs[jhi - j0]
                slot = kb % 8
                u = rot["u"]
                rot["u"] += 1
                E = EB[u % 2]
                P = PTb[u % 3]
                first_pair = ui == 0

                def s0(E=E, nk=nk, c0=c0, c1=c1, slot=slot, po=po, hp=hp):
                    S.op("pe", lambda e: e.matmul(out=E.ap[0:nk, c0:c1], lhsT=KTA.ap[:, hp, slot * 128:slot * 128 + nk],
                                                  rhs=QT.ap[:, 2 * hp + po // 64, c0:c1], start=True, stop=True), reads=[KTA, QT], writes=[E])

                def s0b(E=E, nk=nk, h=h, kb=kb, jlo=jlo, jhi=jhi):
                    js = list(range(jlo, jhi + 1))
                    if kb in js and kb + 1 in js and subs[kb - j0] == 128 and subs[kb + 1 - j0] == 128:
                        cj = 128 * (kb - j0)
                        S.op("pe", lambda e, cj=cj: e.matmul(out=E.ap[0:nk, cj:cj + 256], lhsT=ident.ap[0:nk, 0:nk],
                                                             rhs=biasT.ap[0:nk, h].rearrange("p a b -> p (a b)"), start=False, stop=True, skip_group_check=True),
                             reads=[ident, biasT], writes=[E])
                        js = [j for j in js if j not in (kb, kb + 1)]
                    for j in js:
                        d = j - kb
                        if d not in (0, 1, 4):
                            continue
                        cj = 128 * (j - j0)
                        nj = subs[j - j0]
                        if d == 4:
                            S.op("pe", lambda e, cj=cj, nj=nj: e.matmul(out=E.ap[0:nk, cj:cj + nj], lhsT=ident.ap[0:nk, 0:nk], rhs=bias4.ap[0:nk, 0:nj],
                                                                        start=False, stop=True, skip_group_check=True), reads=[ident, bias4], writes=[E])
                        else:
                            S.op("pe", lambda e, cj=cj, nj=nj, d=d: e.matmul(out=E.ap[0:nk, cj:cj + nj], lhsT=ident.ap[0:nk, 0:nk], rhs=biasT.ap[0:nk, h, d, 0:nj],
                                                                             start=False, stop=True, skip_group_check=True), reads=[ident, biasT], writes=[E])

                def s1(E=E, P=P, nk=nk, c0=c0, c1=c1, h=h):
                    S.op("act", lambda e: e.activation(out=P.ap[0:nk, c0:c1], in_=E.ap[0:nk, c0:c1], func=AF.Exp), reads=[E], writes=[P])

                def s2(P=P, nk=nk, c0=c0, c1=c1, slot=slot, po=po, h=h, first=first, PO=PO, PDEN=PDEN, first_pair=first_pair):
                    hp_ = h // 2
                    PO_, PD_ = PO[h % 2], PDEN
                    S.op("pe", lambda e: e.matmul(out=PO_.ap[:, c0:c1], lhsT=VA.ap[0:nk, slot, hp_ * 128:(hp_ + 1) * 128], rhs=P.ap[0:nk, c0:c1],
                                                  start=first, stop=True, skip_group_check=True), reads=[VA, P], writes=[PO_])
                    S.op("pe", lambda e: e.matmul(out=PD_.ap[:, c0:c1], lhsT=onesLR.ap[0:nk, h % 2, :], rhs=P.ap[0:nk, c0:c1],
                                                  start=first_pair, stop=True, skip_group_check=True), reads=[onesLR, P], writes=[PD_])

                s3 = None
                if last:
                    def s3(hp=hp, PO=PO, PDEN=PDEN):
                        R = e32[hp % 2]
                        S.op("dve", lambda e: e.reciprocal(out=R.ap[:, 0:W], in_=PDEN.ap[:, 0:W]), reads=[PDEN], writes=[R])
                        for e_ in range(2):
                            S.op("dve", lambda e, e_=e_: e.tensor_tensor(out=R.ap[64 * e_:64 * e_ + 64, 0:W], in0=PO[e_].ap[64 * e_:64 * e_ + 64, 0:W],
                                                                        in1=R.ap[64 * e_:64 * e_ + 64, 0:W], op=ALU.mult), reads=[PO[e_], R], writes=[R])
                        S.op("dve", lambda e: e.tensor_tensor(out=hnT.ap[:, hp, 0:W], in0=R.ap[:, 0:W], in1=sgT.ap[:, hp, 0:W], op=ALU.mult),
                             reads=[R, sgT], writes=hs)
                units.append([(lambda a=s0, b=s0b: (a(), b())), s1, None, s2, None, s3])

        pipeline(units, 6, order=[1, 2, 3, 4, 5, 0], skews=[0, 1, 1, 2, 2, 3])
        units = []
        eeP = xpair("att", "e32_0", F32)
        spP = [xpair("att", "spb_0", BF16), xpair("att", "spb_2", BF16)]
        ptP = [xpair("att", "PT_0", BF16), xpair("att", "PT_2", BF16)]
        ssP = xpair("att", "SS_0", BF16)
        spB = [[v["spb_0"], v["spb_1"]], [v["spb_2"], v["spb_3"]]]
        ptB = [[v["PT_0"], v["PT_1"]], [v["PT_2"], v["PT_3"]]]
        ssB = [v["SS_0"], v["SS_1"]]
        for hp in range(4):
            PO = [BK[6], BK[7]]
            kbs = list(range(j0 + nsub - 1, -1, -1))
            for ui, kb in enumerate(kbs):
                first = ui == 0
                last = ui == len(kbs) - 1
                nk = nkeys(kb)
                diag = kb >= j0
                c0 = 128 * (kb - j0) if diag else 0
                nd = subs[kb - j0] if diag else 0
                u = rot["u"]
                rot["u"] += 1
                q = u % 2
                qe = u % 3
                E2 = [BK[2 * qe], BK[2 * qe + 1]]
                Eap = BIG[0:nk, 2 * qe:2 * qe + 2, c0:W]
                sp2, spap = spB[q], spP[q]
                pt2, ptap = ptB[q], ptP[q]

                def s0(E2=E2, nk=nk, c0=c0, kb=kb, hp=hp):
                    for e_ in range(2):
                        S.op("pe", lambda e, e_=e_: e.matmul(out=E2[e_].ap[0:nk, c0:W], lhsT=KTB.ap[:, hp, kb * 128:kb * 128 + nk],
                                                             rhs=QT.ap[:, 8 + 2 * hp + e_, c0:W], start=True, stop=True), reads=[KTB, QT], writes=[E2[e_]])

                def s1a(E2=E2, Eap=Eap, nk=nk, c0=c0):
                    S.op("act", lambda e: e.activation(out=eeP[0:nk, :, c0:W], in_=Eap, func=AF.Exp), reads=E2, writes=e32)

                def s1b(sp2=sp2, spap=spap, nk=nk, c0=c0, nd=nd, diag=diag):
                    S.op("act", lambda e: e.activation(out=spap[0:nk, :, c0:W], in_=eeP[0:nk, :, c0:W], func=AF.Ln, bias=1.0), reads=e32, writes=sp2)
                    if diag:
                        for e_ in range(2):
                            S.op("dve", lambda e, e_=e_: e.tensor_tensor(out=sp2[e_].ap[0:nk, c0:c0 + nd], in0=sp2[e_].ap[0:nk, c0:c0 + nd],
                                                                        in1=mdiag.ap[0:nk, 0:nd], op=ALU.mult), reads=[sp2[e_], mdiag], writes=[sp2[e_]])

                def s2a(E2=E2, sp2=sp2, spap=spap, nk=nk, c0=c0, kb=kb, first=first):
                    for e_ in range(2):
                        S.op("pe", lambda e, e_=e_: e.matmul(out=E2[e_].ap[0:nk, c0:W], lhsT=triM.ap[0:nk, 0:nk], rhs=sp2[e_].ap[0:nk, c0:W], start=False, stop=True,
                                                             skip_group_check=True), reads=[triM, sp2[e_]], writes=[E2[e_]])
                    if not first:
                        for e_ in range(2):
                            S.op("pe", lambda e, e_=e_: e.matmul(out=E2[e_].ap[0:nk, c0:W], lhsT=nones.ap[:, 0:nk], rhs=ssB[e_].ap[:, c0:W], start=False, stop=True,
                                                                 skip_group_check=True), reads=[nones, ssB[e_]], writes=[E2[e_]])
                    if kb > 0:
                        if first:
                            S.op("dve", lambda e: e.memset(ssP[:, :, 0:W], 0.0), writes=ssB)
                        S.op("dve", lambda e: e.tensor_tensor(out=ssP[0:nk, :, c0:W], in0=ssP[0:nk, :, c0:W], in1=spap[0:nk, :, c0:W], op=ALU.add),
                             reads=ssB + sp2, writes=ssB)

                def s2b(E2=E2, Eap=Eap, pt2=pt2, ptap=ptap, nk=nk, c0=c0, nd=nd, diag=diag):
                    S.op("act", lambda e: e.activation(out=ptap[0:nk, :, c0:W], in_=Eap, func=AF.Exp), reads=E2, writes=pt2)
                    if diag:
                        for e_ in range(2):
                            S.op("dve", lambda e, e_=e_: e.tensor_tensor(out=pt2[e_].ap[0:nk, c0:c0 + nd], in0=pt2[e_].ap[0:nk, c0:c0 + nd],
                                                                        in1=mdiag.ap[0:nk, 0:nd], op=ALU.mult), reads=[pt2[e_], mdiag], writes=[pt2[e_]])

                def s3(pt2=pt2, nk=nk, c0=c0, kb=kb, hp=hp, first=first, last=last, PO=PO):
                    for e_ in range(2):
                        S.op("pe", lambda e, e_=e_: e.matmul(out=PO[e_].ap[:, c0:W], lhsT=VB.ap[0:nk, kb, hp * 128:(hp + 1) * 128], rhs=pt2[e_].ap[0:nk, c0:W],
                                                             start=first, stop=True, skip_group_check=True), reads=[VB, pt2[e_]], writes=[PO[e_]])
                    if last:
                        for e_ in range(2):
                            S.op("dve", lambda e, e_=e_: e.tensor_tensor(out=hnT.ap[64 * e_:64 * e_ + 64, 4 + hp, 0:W], in0=PO[e_].ap[64 * e_:64 * e_ + 64, 0:W],
                                                                        in1=sgT.ap[64 * e_:64 * e_ + 64, 4 + hp, 0:W], op=ALU.mult), reads=[PO[e_], sgT], writes=hs)
                units.append([s0, s1a, s1b, s2a, s2b, s3])
        pipeline(units, 6, order=[1, 2, 3, 4, 5, 0], skews=[0, 1, 1, 2, 2, 3])

    def attention_sample(j0, n):
        WH = 8 * n
        v = phase_views("att")
        e32 = [v["e32_0"], v["e32_1"]]
        spb = [v["spb_0"], v["spb_1"], v["spb_2"]]
        PTb = [v["PT_0"], v["PT_1"], v["PT_2"]]
        SS = v["SS_0"]
        hs = hnT_s[0:1]
        units = []

        def nkeys(kb):
            return n if kb >= j0 else 128

        def qk(E, KT, nk, kcol, qbase):
            for h in range(8):
                S.op("pe", lambda e, h=h: e.matmul(out=E.ap[0:nk, n * h:n * (h + 1)], lhsT=KT.ap[:, h // 2, kcol:kcol + nk], rhs=QT.ap[:, qbase + h, 0:n],
                                                   start=(h == 0), stop=True, skip_group_check=True), reads=[KT, QT], writes=[E])

        def gate(POb, chunk0, src_fn):
            for e_ in range(2):
                S.op("dve", lambda e, e_=e_: e.tensor_tensor(out=hnT.ap[64 * e_:64 * e_ + 64, chunk0:chunk0 + 4, 0:n],
                                                            in0=src_fn(e_).rearrange("p (a b c) -> p a b c", a=4, b=2)[:, :, e_, :],
                                                            in1=sgT.ap[64 * e_:64 * e_ + 64, chunk0:chunk0 + 4, 0:n], op=ALU.mult), reads=[POb, sgT], writes=hs)

        PO, PDEN = BK[4], BK[6]
        kbs = list(range(4, j0 + 1))
        for ui, kb in enumerate(kbs):
            nk = nkeys(kb)
            d = j0 - kb
            slot = kb % 8
            u = rot["u"]
            rot["u"] += 1
            E = BK[u % 4]
            P = PTb[u % 3]
            first = ui == 0
            last = ui == len(kbs) - 1

            def s0(E=E, nk=nk, slot=slot, d=d):
                qk(E, KTA, nk, slot * 128, 0)
                if d in (0, 1):
                    for h in range(8):
                        S.op("pe", lambda e, h=h: e.matmul(out=E.ap[0:nk, n * h:n * (h + 1)], lhsT=ident.ap[0:nk, 0:nk], rhs=biasT.ap[0:nk, h, d, 0:n],
                                                           start=False, stop=True, skip_group_check=True), reads=[ident, biasT], writes=[E])

            def s1(E=E, P=P, nk=nk):
                S.op("act", lambda e: e.activation(out=P.ap[0:nk, 0:WH], in_=E.ap[0:nk, 0:WH], func=AF.Exp), reads=[E], writes=[P])

            def s2(P=P, nk=nk, slot=slot, first=first):
                for h in range(8):
                    S.op("pe", lambda e, h=h: e.matmul(out=PO.ap[:, n * h:n * (h + 1)], lhsT=VA.ap[0:nk, slot, (h // 2) * 128:(h // 2 + 1) * 128],
                                                       rhs=P.ap[0:nk, n * h:n * (h + 1)], start=(first and h == 0), stop=True, skip_group_check=True),
                         reads=[VA, P], writes=[PO])
                for h in range(8):
                    for e_ in range(2):
                        S.op("pe", lambda e, h=h, e_=e_: e.matmul(out=PDEN.ap[:, n * h:n * (h + 1)], lhsT=onesLR.ap[0:nk, e_, :], rhs=P.ap[0:nk, n * h:n * (h + 1)],
                                                                  start=(first and h == 0 and e_ == 0), stop=True, skip_group_check=True),
                             reads=[onesLR, P], writes=[PDEN])

            s3 = None
            if last:
                def s3():
                    R = e32[0]
                    S.op("dve", lambda e: e.reciprocal(out=R.ap[:, 0:WH], in_=PDEN.ap[:, 0:WH]), reads=[PDEN], writes=[R])
                    S.op("dve", lambda e: e.tensor_tensor(out=R.ap[:, 0:WH], in0=PO.ap[:, 0:WH], in1=R.ap[:, 0:WH], op=ALU.mult), reads=[PO, R], writes=[R])
                    gate(R, 0, lambda e_: R.ap[64 * e_:64 * e_ + 64, 0:WH])
            units.append([s0, s1, None, s2, None, s3])

        POB = BK[5]
        kbs = list(range(j0, -1, -1))
        for ui, kb in enumerate(kbs):
            nk = nkeys(kb)
            diag = kb >= j0
            u = rot["u"]
            rot["u"] += 1
            E = BK[u % 4]
            P = PTb[u % 3]
            sp = spb[u % 3]
            ee = e32[u % 2]
            first = ui == 0
            last = ui == len(kbs) - 1

            def s0(E=E, nk=nk, kb=kb):
                qk(E, KTB, nk, kb * 128, 8)

            def s1a(E=E, ee=ee, nk=nk):
                S.op("act", lambda e: e.activation(out=ee.ap[0:nk, 0:WH], in_=E.ap[0:nk, 0:WH], func=AF.Exp), reads=[E], writes=[ee])

            def s1b(ee=ee, sp=sp, nk=nk, diag=diag):
                S.op("act", lambda e: e.activation(out=sp.ap[0:nk, 0:WH], in_=ee.ap[0:nk, 0:WH], func=AF.Ln, bias=1.0), reads=[ee], writes=[sp])
                if diag:
                    S.op("dve", lambda e: e.tensor_tensor(out=sp.ap[0:nk, 0:WH], in0=sp.ap[0:nk, 0:WH], in1=mdiag8.ap[0:nk, 0:WH], op=ALU.mult),
                         reads=[sp, mdiag8], writes=[sp])

            def s2a(E=E, sp=sp, nk=nk, kb=kb, first=first):
                S.op("pe", lambda e: e.matmul(out=E.ap[0:nk, 0:WH], lhsT=triM.ap[0:nk, 0:nk], rhs=sp.ap[0:nk, 0:WH], start=False, stop=True,
                                              skip_group_check=True), reads=[triM, sp], writes=[E])
                if not first:
                    S.op("pe", lambda e: e.matmul(out=E.ap[0:nk, 0:WH], lhsT=nones.ap[:, 0:nk], rhs=SS.ap[:, 0:WH], start=False, stop=True,
                                                  skip_group_check=True), reads=[nones, SS], writes=[E])
                if kb > 0:
                    if first:
                        S.op("dve", lambda e: e.memset(SS.ap[:, 0:WH], 0.0), writes=[SS])
                    S.op("dve", lambda e: e.tensor_tensor(out=SS.ap[0:nk, 0:WH], in0=SS.ap[0:nk, 0:WH], in1=sp.ap[0:nk, 0:WH], op=ALU.add),
                         reads=[SS, sp], writes=[SS])

            def s2b(E=E, P=P, nk=nk, diag=diag):
                S.op("act", lambda e: e.activation(out=P.ap[0:nk, 0:WH], in_=E.ap[0:nk, 0:WH], func=AF.Exp), reads=[E], writes=[P])
                if diag:
                    S.op("dve", lambda e: e.tensor_tensor(out=P.ap[0:nk, 0:WH], in0=P.ap[0:nk, 0:WH], in1=mdiag8.ap[0:nk, 0:WH], op=ALU.mult),
                         reads=[P, mdiag8], writes=[P])

            def s3(P=P, nk=nk, kb=kb, first=first, last=last):
                for h in range(8):
                    S.op("pe", lambda e, h=h: e.matmul(out=POB.ap[:, n * h:n * (h + 1)], lhsT=VB.ap[0:nk, kb, (h // 2) * 128:(h // 2 + 1) * 128],
                                                       rhs=P.ap[0:nk, n * h:n * (h + 1)], start=(first and h == 0), stop=True, skip_group_check=True),
                         reads=[VB, P], writes=[POB])
                if last:
                    gate(POB, 4, lambda e_: POB.ap[64 * e_:64 * e_ + 64, 0:WH])
            units.append([s0, s1a, s1b, s2a, s2b, s3])
        pipeline(units, 6, order=[1, 4, 2, 3, 5, 0], skews=[0, 1, 1, 2, 3, 4])

    def phase3(subs, hsrc, hdst, psrc):
        nsub = len(subs)
        v = phase_views("p3")
        hfs = [hf, v["hfB"]]
        sig = v["sig"]
        hTs = [v["hT_0"], v["hT_1"]]
        pfs = [v["pf_0"], v["pf_1"]]
        pb = v["pb"]
        pTs = [v["pT_0"], v["pT_1"]]
        units = []
        for i, n in enumerate(subs):
            h_ = hfs[i % 2]
            pf = pfs[i % 2]
            hT = hTs[i % 2]
            pT = pTs[i % 2]
            sd = st[i % 2]
            yb = [BK[0], BK[1]] if i % 2 == 0 else [BK[2], BK[3]]
            gbs = [(BK[4], BK[5]), (BK[6], BK[7])]

            def p0(i=i, n=n, yb=yb):
                for hlf in range(2):
                    for kc in range(8):
                        S.op("pe", lambda e, kc=kc, hlf=hlf: e.matmul(out=yb[hlf].ap[0:n, :], lhsT=hnT.ap[:, kc, 128 * i:128 * i + n],
                                                                      rhs=Wout.ap[:, kc, hlf * 512:(hlf + 1) * 512], start=(kc == 0), stop=(kc == 7)),
                             reads=[hnT_s[i], Wout], writes=[yb[hlf]])

            def ld(i=i, n=n, h_=h_, pf=pf):
                S.dma("sp", h_.ap[0:n], hsrc(i), writes=[h_])
                S.dma("sp", pf.ap[0:n], psrc(i), writes=[pf])

            def p1a(i=i, n=n, h_=h_, pf=pf, sd=sd, yb=yb):
                for hlf in range(2):
                    S.op("act", lambda e, hlf=hlf: e.activation(out=kbf.ap[0:n, :], in_=yb[hlf].ap[0:n, :], func=AF.Square,
                                                                accum_out=sd["ss"].ap[0:n, hlf:hlf + 1]), reads=[yb[hlf]], writes=[kbf, sd["ss"]])
                S.op("dve", lambda e: e.tensor_tensor(out=sd["s1"].ap[0:n, 0:1], in0=sd["ss"].ap[0:n, 0:1], in1=sd["ss"].ap[0:n, 1:2], op=ALU.add),
                     reads=[sd["ss"]], writes=[sd["s1"]])
                rstd_from(sd, "s1", n)
                for hlf in range(2):
                    S.op("dve", lambda e, hlf=hlf: e.scalar_tensor_tensor(out=yb[hlf].ap[0:n, :], in0=yb[hlf].ap[0:n, :], scalar=sd["r"].ap[0:n, 0:1],
                                                                         in1=gpost.ap[0:n, hlf * 512:(hlf + 1) * 512], op0=ALU.mult, op1=ALU.mult),
                         reads=[yb[hlf], sd["r"], gpost], writes=[yb[hlf]])
                    S.op("dve", lambda e, hlf=hlf: e.tensor_tensor(out=h_.ap[0:n, hlf * 512:(hlf + 1) * 512], in0=h_.ap[0:n, hlf * 512:(hlf + 1) * 512],
                                                                  in1=yb[hlf].ap[0:n, :], op=ALU.add), reads=[h_, yb[hlf]], writes=[h_])
                S.op("dve", lambda e: e.tensor_copy(out=pb.ap[0:n], in_=pf.ap[0:n]), reads=[pf], writes=[pb])

            def p1c(n=n, h_=h_):
                S.op("act", lambda e: e.activation(out=hn.ap[0:n], in_=h_.ap[0:n], func=AF.Copy), reads=[h_], writes=[hn])

            def p1b(n=n, hT=hT, pT=pT, yb=yb):
                transpose_to(hn, n, 8, lambda: hT.ap[:, :, 0:n], [hT], ptr=yb[0])
                transpose_to(pb, n, 2, lambda: pT.ap[:, :, 0:n], [pT], ptr=yb[1])

            def p3pe(n=n, hT=hT, pT=pT, gbs=gbs):
                for hlf in range(2):
                    gb, pk = gbs[hlf]
                    for kc in range(8):
                        S.op("pe", lambda e, kc=kc, hlf=hlf, gb=gb: e.matmul(out=gb.ap[0:n, :], lhsT=hT.ap[:, kc, 0:n], rhs=Wgate.ap[:, kc, hlf * 512:(hlf + 1) * 512],
                                                                             start=(kc == 0), stop=(kc == 7)), reads=[hT, Wgate], writes=[gb])
                    for kc in range(2):
                        S.op("pe", lambda e, kc=kc, hlf=hlf, pk=pk: e.matmul(out=pk.ap[0:n, :], lhsT=pT.ap[:, kc, 0:n], rhs=Wple.ap[:, kc, hlf * 512:(hlf + 1) * 512],
                                                                             start=(kc == 0), stop=(kc == 1)), reads=[pT, Wple], writes=[pk])

            def p3ev(i=i, n=n, h_=h_, gbs=gbs):
                for hlf in range(2):
                    gb, pk = gbs[hlf]
                    S.op("act", lambda e, gb=gb: e.activation(out=sig.ap[0:n, :], in_=gb.ap[0:n, :], func=AF.Sigmoid), reads=[gb], writes=[sig])
                    S.op("dve", lambda e, pk=pk: e.tensor_tensor(out=sig.ap[0:n, :], in0=pk.ap[0:n, :], in1=sig.ap[0:n, :], op=ALU.mult),
                         reads=[pk, sig], writes=[sig])
                    S.op("pool", lambda e, hlf=hlf: e.tensor_tensor(out=h_.ap[0:n, hlf * 512:(hlf + 1) * 512], in0=h_.ap[0:n, hlf * 512:(hlf + 1) * 512],
                                                                   in1=sig.ap[0:n, :], op=ALU.add), reads=[h_, sig], writes=[h_])
                S.dma("sp", hdst(i), h_.ap[0:n], reads=[h_])

            units.append([p0, ld, p1a, p1c, p1b, p3pe, p3ev])
        pipeline(units, 7, order=[2, 5, 0, 6, 3, 4, 1], skews=[0, 0, 1, 1, 1, 2, 2])

    def supertile(l, j0, subs, hsrc, hdst, psrc, kvout, after_p1=None):
        phase1(l, j0, subs, hsrc, kvout)
        if after_p1 is not None:
            after_p1()
        if state["sample"]:
            attention_sample(j0, subs[0])
        else:
            attention(j0, subs)
        phase3(subs, hsrc, hdst, psrc)

    def load_cache(l, s):
        v = phase_views("p1")
        kbfs = [kbf, v["kbf2"]]
        stgs = list(stage)
        for nm in ("h1", "h2", "h3"):
            for hh in range(2):
                stgs.append(Buf(nm + "s%d" % hh, v[nm].ap[:, hh * 512:(hh + 1) * 512]))
        S.alias(stgs[2:], state["xbufs"])
        state["xbufs"] = state["xbufs"] + stgs[2:]
        k = 0
        jobs = []
        for (ck, cv, KT, VV, blks, off) in [(cache_b_k, cache_b_v, KTB, VB, range(8), 0), (cache_a_k, cache_a_v, KTA, VA, range(4, 8), 4)]:
            for blk in blks:
                jobs.append((ck, KT, blk, off, True))
                jobs.append((cv, VV, blk, off, False))
        AHEAD = 6
        loaded = []
        for ji in range(len(jobs) + AHEAD):
            if ji < len(jobs):
                src, dstb, blk, off, isk = jobs[ji]
                stg = stgs[ji % len(stgs)]
                S.dma("sp", stg.ap[:], src[l, s, (blk - off) * 128:(blk - off + 1) * 128, :], writes=[stg])
                loaded.append(stg)
            jj = ji - AHEAD
            if jj >= 0:
                src, dstb, blk, off, isk = jobs[jj]
                stg = loaded[jj]
                if isk:
                    kb2 = kbfs[k % 2]
                    k += 1
                    S.op("dve", lambda e, stg=stg, kb2=kb2: e.tensor_copy(out=kb2.ap[:], in_=stg.ap[:]), reads=[stg], writes=[kb2])
                    transpose_to(kb2, 128, 4, lambda blk=blk, KT=dstb: KT.ap[:, :, blk * 128:(blk + 1) * 128], [dstb])
                else:
                    S.op("act", lambda e, stg=stg, blk=blk, VV=dstb: e.activation(out=VV.ap[:, blk, :], in_=stg.ap[:], func=AF.Copy), reads=[stg], writes=[dstb])

    load_win(0)
    for l in range(DEPTH):
        load_rest(l)
        src_p = x_prompt if l == 0 else h1p
        dst_p = h1p if l == 0 else y_prompt
        src_s = x_sample if l == 0 else h1s
        dst_s = h1s if l == 0 else y_sample
        state["sample"] = True
        for s in range(NS):
            load_cache(l, s)
            supertile(l, 8, [DSEQ],
                      lambda i, s=s: src_s[s, :, :],
                      lambda i, s=s: dst_s[s, :, :],
                      lambda i, s=s, l=l: p_sample[l, s, :, :],
                      lambda i, s=s, l=l: {"ka": sak[l, s, :, :], "va": sav[l, s, :, :], "kb": sbk[l, s, :, :], "vb": sbv[l, s, :, :]})
        state["sample"] = False
        for s in range(NB):
            for T in range(4):
                def kvout(i, T=T, s=s, l=l):
                    t0 = 512 * T + 128 * i
                    dct = {"kb": pbk[l, s, t0:t0 + 128, :], "vb": pbv[l, s, t0:t0 + 128, :], "ka": None, "va": None}
                    if t0 >= SEQ - 512:
                        dct["ka"] = pak[l, s, t0 - (SEQ - 512):t0 - (SEQ - 512) + 128, :]
                        dct["va"] = pav[l, s, t0 - (SEQ - 512):t0 - (SEQ - 512) + 128, :]
                    return dct
                last_p1 = (s == NB - 1 and T == 3 and l + 1 < DEPTH)
                supertile(l, 4 * T, [128] * 4,
                          lambda i, T=T, s=s: src_p[s, 512 * T + 128 * i:512 * T + 128 * (i + 1), :],
                          lambda i, T=T, s=s: dst_p[s, 512 * T + 128 * i:512 * T + 128 * (i + 1), :],
                          lambda i, T=T, s=s, l=l: p_prompt[l, s, 512 * T + 128 * i:512 * T + 128 * (i + 1), :],
                          kvout, after_p1=((lambda l=l: load_win(l + 1)) if last_p1 else None))
    S.finish()
    return nc, es


_CACHE = {}


def kernel(x_prompt, x_sample, p_prompt, p_sample, cache_a_k, cache_a_v, cache_b_k, cache_b_v,
           g_pre, w_in, rel_bias, w_out, g_post, w_ple, w_ple_gate):
    f = lambda a: np.ascontiguousarray(np.asarray(a, dtype=np.float32))
    x_prompt, x_sample, p_prompt, p_sample = f(x_prompt), f(x_sample), f(p_prompt), f(p_sample)
    cache_a_k, cache_a_v, cache_b_k, cache_b_v = f(cache_a_k), f(cache_a_v), f(cache_b_k), f(cache_b_v)
    rel_bias = f(rel_bias)
    s_ = np.arange(128)[:, None]
    t_ = np.arange(128)[None, :]
    idx0 = np.clip(s_ - t_, -128, 128) + 128
    idx1 = np.clip(s_ - t_ - 128, -128, 128) + 128
    bt = np.stack([rel_bias[:, :, idx0], rel_bias[:, :, idx1]], axis=2)
    bias_t = np.ascontiguousarray(bt.transpose(0, 3, 1, 2, 4)).reshape(DEPTH, 128, 8 * 2 * 128)
    bias_c = np.ascontiguousarray(rel_bias[:, :, 0])

    if "nc" not in _CACHE:
        _CACHE["nc"] = build_program()
    nc, es = _CACHE["nc"]
    in_maps = []
    for c in range(NCORES):
        b0, b1 = NB * c, NB * (c + 1)
        in_maps.append({
            "x_prompt": x_prompt[b0:b1], "x_sample": x_sample[b0:b1],
            "p_prompt": np.ascontiguousarray(p_prompt[:, b0:b1]), "p_sample": np.ascontiguousarray(p_sample[:, b0:b1]),
            "cache_a_k": np.ascontiguousarray(cache_a_k[:, b0:b1]).reshape(DEPTH, NS, 512, 512),
            "cache_a_v": np.ascontiguousarray(cache_a_v[:, b0:b1]).reshape(DEPTH, NS, 512, 512),
            "cache_b_k": np.ascontiguousarray(cache_b_k[:, b0:b1]).reshape(DEPTH, NS, PAST, 512),
            "cache_b_v": np.ascontiguousarray(cache_b_v[:, b0:b1]).reshape(DEPTH, NS, PAST, 512),
            "g_pre": f(g_pre), "g_post": f(g_post), "w_in": f(w_in), "w_out": f(w_out), "w_ple": f(w_ple),
            "w_ple_gate": f(w_ple_gate), "bias_t": bias_t, "bias_c": bias_c,
        })
    res = run_bass_kernel_spmd(nc, in_maps, core_ids=list(range(NCORES)))
    R = res.results
    cat0 = lambda k: np.concatenate([r[k] for r in R], axis=0)
    cat1 = lambda k, shp: np.concatenate([r[k] for r in R], axis=1).reshape(shp)
    B = NB * NCORES
    return (cat0("y_prompt"), cat0("y_sample"),
            cat1("pak", (DEPTH, B, 512, 8, 64)), cat1("pav", (DEPTH, B, 512, 8, 64)),
            cat1("pbk", (DEPTH, B, SEQ, 8, 64)), cat1("pbv", (DEPTH, B, SEQ, 8, 64)),
            cat1("sak", (DEPTH, B, DSEQ, 8, 64)), cat1("sav", (DEPTH, B, DSEQ, 8, 64)),
            cat1("sbk", (DEPTH, B, DSEQ, 8, 64)), cat1("sbv", (DEPTH, B, DSEQ, 8, 64)))
```
